# Optimizing a Trainium2 kernel written in Bass

```python
import jax, jax.numpy as jnp
from jax import lax
import numpy as np

D_MODEL = 1024
BATCH = 8
SEQ = 8192
DEPTH = 1
DEC_BATCH = 8
DEC_SEQ = 16
PAST_LEN = 1024

CHUNK = 64
POOL_WINDOWS = (2, 4, 8, 16)
N_POOL_GROUPS = len(POOL_WINDOWS)
POOL_WIDTH = D_MODEL // 2
POOL_GROUP = POOL_WIDTH // N_POOL_GROUPS
CONV_WIDTH = D_MODEL - POOL_WIDTH
CONV_KERNEL = 31
POOL_HIST = max(POOL_WINDOWS) - 1
CONV_HIST = CONV_KERNEL - 1
D_FF = 4 * D_MODEL
LN_EPS = 1e-5
DN_ALPHA = (2.0 * DEPTH) ** 0.25
DN_BETA = (8.0 * DEPTH) ** -0.25

kernel_name = "pool_conformer_hybrid_stream_step"


def layer_norm(x, g, b):
    xf = x.astype(jnp.float32)
    mu = jnp.mean(xf, axis=-1, keepdims=True)
    var = jnp.mean(jnp.square(xf - mu), axis=-1, keepdims=True)
    y = (xf - mu) * lax.rsqrt(var + LN_EPS) * g.astype(jnp.float32) + b.astype(jnp.float32)
    return y.astype(x.dtype)


def pool_mixer(u_ext, pos0, w_pool, pool_scale):
    bsz, ext_len, _ = u_ext.shape
    seq = ext_len - POOL_HIST
    uf = u_ext.astype(jnp.float32)
    cs = jnp.pad(jnp.cumsum(uf, axis=1), ((0, 0), (1, 0), (0, 0)))
    pos = pos0 + jnp.arange(seq)
    end = cs[:, POOL_HIST + 1:]
    u_new = uf[:, POOL_HIST:]
    outs = []
    for gi, w in enumerate(POOL_WINDOWS):
        sl = slice(gi * POOL_GROUP, (gi + 1) * POOL_GROUP)
        start = cs[:, POOL_HIST + 1 - w: POOL_HIST + 1 - w + seq, sl]
        count = jnp.minimum(pos + 1, w).astype(jnp.float32)[None, :, None]
        outs.append((end[..., sl] - start) / count - u_new[..., sl])
    d = jnp.stack(outs, axis=2).astype(u_ext.dtype)
    y = jnp.einsum('blgc,gcd->blgd', d, w_pool)
    return y.reshape(bsz, seq, POOL_WIDTH) * pool_scale


def conv_mixer(v_ext, conv_w, conv_b, ln_g, ln_b):
    y = lax.conv_general_dilated(v_ext, conv_w, window_strides=(1,), padding='VALID',
                                 dimension_numbers=('NWC', 'WIO', 'NWC'),
                                 feature_group_count=CONV_WIDTH)
    y = layer_norm(y + conv_b, ln_g, ln_b)
    return jax.nn.silu(y)


def trunk_layer(x, hist_pool, hist_conv, pos0, w_in, b_in, w_pool, pool_scale, conv_w, conv_b,
                conv_ln_g, conv_ln_b, w_out, b_out, ln1_g, ln1_b, w_up, w_down, ln2_g, ln2_b):
    z = x @ w_in + b_in
    u = z[..., :POOL_WIDTH]
    a = z[..., POOL_WIDTH:POOL_WIDTH + CONV_WIDTH]
    g = z[..., POOL_WIDTH + CONV_WIDTH:]
    v = a * jax.nn.sigmoid(g)
    u_ext = jnp.concatenate([hist_pool, u], axis=1)
    v_ext = jnp.concatenate([hist_conv, v], axis=1)
    mix = jnp.concatenate([pool_mixer(u_ext, pos0, w_pool, pool_scale),
                           conv_mixer(v_ext, conv_w, conv_b, conv_ln_g, conv_ln_b)], axis=-1)
    h = layer_norm(DN_ALPHA * x + mix @ w_out + b_out, ln1_g, ln1_b)
    f = jnp.square(jax.nn.relu(h @ w_up)) @ w_down
    y = layer_norm(DN_ALPHA * h + f, ln2_g, ln2_b)
    return y, u_ext[:, -POOL_HIST:], v_ext[:, -CONV_HIST:]


def setup_inputs(seed: int = 0) -> dict:
    key = jax.random.key(seed)
    ks = jax.random.split(key, 20)
    f32 = jnp.float32
    n = lambda k, s, sc: jax.random.normal(k, s, f32) * sc
    d_in = POOL_WIDTH + 2 * CONV_WIDTH
    return {
        "x_prompt": n(ks[0], (BATCH, SEQ, D_MODEL), 1.0),
        "x_sample": n(ks[1], (DEC_BATCH, DEC_SEQ, D_MODEL), 1.0),
        "state_pool": n(ks[2], (DEPTH, DEC_BATCH, POOL_HIST, POOL_WIDTH), 1.0),
        "state_conv": n(ks[3], (DEPTH, DEC_BATCH, CONV_HIST, CONV_WIDTH), 0.5),
        "w_in": n(ks[4], (DEPTH, D_MODEL, d_in), D_MODEL ** -0.5),
        "b_in": n(ks[5], (DEPTH, d_in), 0.02),
        "w_pool": n(ks[6], (DEPTH, N_POOL_GROUPS, POOL_GROUP, POOL_GROUP), POOL_GROUP ** -0.5),
        "pool_scale": 1.0 + n(ks[7], (DEPTH, POOL_WIDTH), 0.1),
        "conv_w": n(ks[8], (DEPTH, CONV_KERNEL, 1, CONV_WIDTH), CONV_KERNEL ** -0.5),
        "conv_b": n(ks[9], (DEPTH, CONV_WIDTH), 0.02),
        "conv_ln_g": 1.0 + n(ks[10], (DEPTH, CONV_WIDTH), 0.05),
        "conv_ln_b": n(ks[11], (DEPTH, CONV_WIDTH), 0.02),
        "w_out": n(ks[12], (DEPTH, D_MODEL, D_MODEL), DN_BETA * D_MODEL ** -0.5),
        "b_out": n(ks[13], (DEPTH, D_MODEL), 0.02),
        "ln1_g": 1.0 + n(ks[14], (DEPTH, D_MODEL), 0.05),
        "ln1_b": n(ks[15], (DEPTH, D_MODEL), 0.02),
        "w_up": n(ks[16], (DEPTH, D_MODEL, D_FF), D_MODEL ** -0.5),
        "w_down": n(ks[17], (DEPTH, D_FF, D_MODEL), DN_BETA * D_FF ** -0.5),
        "ln2_g": 1.0 + n(ks[18], (DEPTH, D_MODEL), 0.05),
        "ln2_b": n(ks[19], (DEPTH, D_MODEL), 0.02),
    }


def reference(x_prompt, x_sample, state_pool, state_conv, w_in, b_in, w_pool, pool_scale,
              conv_w, conv_b, conv_ln_g, conv_ln_b, w_out, b_out, ln1_g, ln1_b,
              w_up, w_down, ln2_g, ln2_b):
    yp, ys = x_prompt, x_sample
    pool_p, conv_p, pool_s, conv_s = [], [], [], []
    for l in range(DEPTH):
        params = (w_in[l], b_in[l], w_pool[l], pool_scale[l], conv_w[l], conv_b[l],
                  conv_ln_g[l], conv_ln_b[l], w_out[l], b_out[l], ln1_g[l], ln1_b[l],
                  w_up[l], w_down[l], ln2_g[l], ln2_b[l])
        zp_pool = jnp.zeros((yp.shape[0], POOL_HIST, POOL_WIDTH), yp.dtype)
        zp_conv = jnp.zeros((yp.shape[0], CONV_HIST, CONV_WIDTH), yp.dtype)
        yp, hp, hc = trunk_layer(yp, zp_pool, zp_conv, 0, *params)
        pool_p.append(hp)
        conv_p.append(hc)
        ys, sp, sc = trunk_layer(ys, state_pool[l].astype(ys.dtype), state_conv[l].astype(ys.dtype),
                                 PAST_LEN, *params)
        pool_s.append(sp)
        conv_s.append(sc)
    return (yp, ys, jnp.stack(pool_p), jnp.stack(conv_p), jnp.stack(pool_s), jnp.stack(conv_s))
```

```python
import contextlib
import numpy as np
import concourse.bass as bass
import concourse.mybir as mybir
from concourse.bass_utils import run_bass_kernel_spmd

F32 = mybir.dt.float32
BF16 = mybir.dt.bfloat16
ALU = mybir.AluOpType
AF = mybir.ActivationFunctionType

D = 1024
DFF = 4096
PW = 512
CW = 512
DIN = 1536
KC = 31
PH = 15
CH = 30
WINS = (2, 4, 8, 16)
ALPHA = 2.0 ** 0.25
EPS = 1e-5
TT = 512
NBLK = 21

P_BIN, P_PSC, P_CB, P_CLG, P_CLB, P_BOUT, P_G1, P_B1, P_G2, P_B2, P_CW = 0, 12, 16, 20, 24, 28, 36, 44, 52, 60, 68
NPAR = P_CW + 4 * KC


class Sem:
    def __init__(self, nc, stack, name, step):
        self.h = stack.enter_context(nc.semaphore(name))
        self.step = step
        self.n = 0

    def next(self):
        self.n += self.step
        return (self, self.n)


class Buf:
    __slots__ = ("w", "r", "name", "excl")

    def __init__(self, name="", excl=False):
        self.w = None
        self.r = {}
        self.name = name
        self.excl = excl


class Q:
    def __init__(self, nc, stack, eng, name):
        self.eng = eng
        self.name = name
        self.sem = Sem(nc, stack, "q_" + name, 1)
        self.seen = {}
        self.nwaits = 0

    def wait_tok(self, tok):
        if tok is None:
            return
        s, n = tok
        if self.seen.get(s, 0) >= n:
            return
        self.seen[s] = n
        self.eng.wait_ge(s.h, n)
        self.nwaits += 1

    def deps(self, reads=(), writes=()):
        for b in reads:
            self.wait_tok(b.w)
            if b.excl:
                for s, n in b.r.items():
                    if s is not self.sem:
                        self.wait_tok((s, n))
        for b in writes:
            self.wait_tok(b.w)
            for s, n in b.r.items():
                self.wait_tok((s, n))

    def commit(self, ins, reads=(), writes=(), sem=None):
        sem = sem or self.sem
        tok = sem.next()
        ins.then_inc(sem.h, sem.step)
        for b in reads:
            if b.r.get(tok[0], 0) < tok[1]:
                b.r[tok[0]] = tok[1]
        for b in writes:
            b.w = tok
            b.r = {}
        return tok

    def op(self, mk, reads=(), writes=(), sem=None):
        self.deps(reads, writes)
        return self.commit(mk(self.eng), reads, writes, sem)


def build(n_ptiles, cfg=None):
    cfg = dict(cfg or {})
    NB = cfg.get("nb", 3)
    NTMP = cfg.get("ntmp", 5)
    sq_on_pool_every = cfg.get("sq_dve_mod", 0)
    SEQ = n_ptiles * TT

    nc = bass.Bass("TRN2", target_bir_lowering=False)
    dt = lambda name, shape, kind, ty=F32: nc.dram_tensor(name, shape, ty, kind=kind).ap()
    xp = dt("xp", [SEQ, D], "ExternalInput")
    xs = dt("xs", [16, D], "ExternalInput")
    sp_in = dt("sp", [PH, PW], "ExternalInput")
    sc_in = dt("sc", [CH, CW], "ExternalInput")
    w_in = dt("w_in", [D, DIN], "ExternalInput")
    w_out = dt("w_out", [D, D], "ExternalInput")
    w_up = dt("w_up", [D, DFF], "ExternalInput")
    w_down = dt("w_down", [DFF, D], "ExternalInput")
    w_pool = dt("w_pool", [4, 128, 128], "ExternalInput")
    params_in = dt("params", [128, NPAR], "ExternalInput")
    ident_in = dt("ident", [128, 128], "ExternalInput")
    rcnt_in = dt("rcnt", [128, 64], "ExternalInput")
    ln2g_in = dt("ln2g", [1, D], "ExternalInput")
    ln2b_in = dt("ln2b", [1, D], "ExternalInput")
    yp = dt("yp", [SEQ, D], "ExternalOutput")
    ys = dt("ys", [16, D], "ExternalOutput")
    o_pp = dt("o_pp", [PH, PW], "ExternalOutput")
    o_cp = dt("o_cp", [CH, CW], "ExternalOutput")
    o_ps = dt("o_ps", [PH, PW], "ExternalOutput")
    o_cs = dt("o_cs", [CH, CW], "ExternalOutput")
    wscr = dt("wscr", [NBLK, 128, 4096], "Internal", BF16)

    stack = contextlib.ExitStack()
    with stack:
        sb = lambda name, shape, ty=F32: stack.enter_context(nc.sbuf_tensor("sb_" + name, shape, ty))
        pe = Q(nc, stack, nc.tensor, "pe")
        act = Q(nc, stack, nc.scalar, "act")
        dve = Q(nc, stack, nc.vector, "dve")
        pool = Q(nc, stack, nc.gpsimd, "pool")
        sp = Q(nc, stack, nc.sync, "sp")

        params = sb("params", [128, NPAR])
        ident = sb("ident", [128, 128])
        rcnt = sb("rcnt", [128, 64])
        ones_bf = sb("ones_bf", [128, 128], BF16)
        epsc = sb("epsc", [128, 8])
        dpar = sb("dpar", [128, 16])
        wpool_bf = sb("wpool_bf", [128, 4, 128], BF16)
        g2bc = sb("g2bc", [128, D])
        b2bc = sb("b2bc", [128, D])
        ystat = sb("ystat", [128, 2, 24])
        b_ystat = [Buf("ystat0"), Buf("ystat1")]
        b_params, b_ident, b_rcnt, b_ones, b_dpar, b_wpool = (Buf(n) for n in ("params", "ident", "rcnt", "ones", "dpar", "wpool"))
        b_g2bc, b_b2bc = Buf("g2bc"), Buf("b2bc")
        CONST = [b_params, b_ident, b_rcnt, b_ones, b_dpar, b_wpool, b_g2bc, b_b2bc]
        misc_n = [0]

        def misc_sem():
            misc_n[0] += 1
            return Sem(nc, stack, f"d_misc{misc_n[0]}", 16)

        def pcol(base, c):
            return params[:, base + c:base + c + 1]

        banks = [stack.enter_context(nc.psum_tensor(f"bank{i}", [128, 512], F32)) for i in range(8)]
        bank_buf = [Buf(f"bank{i}", excl=True) for i in range(8)]
        bank_rr = [0]

        def next_bank():
            i = bank_rr[0] % 6
            bank_rr[0] += 1
            return banks[i], bank_buf[i]

        sp.op(lambda e: e.dma_start(out=params[:], in_=params_in), writes=[b_params], sem=misc_sem())
        sp.op(lambda e: e.dma_start(out=ident[:], in_=ident_in), writes=[b_ident], sem=misc_sem())
        sp.op(lambda e: e.dma_start(out=rcnt[:], in_=rcnt_in), writes=[b_rcnt], sem=misc_sem())
        sp.op(lambda e: e.dma_start(out=g2bc[:], in_=ln2g_in.broadcast_to([128, D])), writes=[b_g2bc], sem=misc_sem())
        sp.op(lambda e: e.dma_start(out=b2bc[:], in_=ln2b_in.broadcast_to([128, D])), writes=[b_b2bc], sem=misc_sem())
        dve.op(lambda e: e.memset(ones_bf[:], 1.0), writes=[b_ones])
        dve.op(lambda e: e.memset(epsc[:], EPS), writes=[b_ones])
        dve.op(lambda e: e.tensor_scalar(out=dpar[:, 0:8], in0=params[:, P_G1:P_G1 + 8], scalar1=ALPHA, scalar2=None,
                                         op0=ALU.mult), reads=[b_params], writes=[b_dpar])
        dve.op(lambda e: e.tensor_scalar(out=dpar[:, 8:16], in0=params[:, P_B1:P_B1 + 8], scalar1=ALPHA, scalar2=None,
                                         op0=ALU.mult), reads=[b_params], writes=[b_dpar])

        blk_src = []
        for j in range(3):
            blk_src.append((w_in.rearrange("(kc p) n -> p kc n", p=128)[:, :, j * 512:(j + 1) * 512], 8, 512))
        for j in range(2):
            blk_src.append((w_out.rearrange("(kc p) n -> p kc n", p=128)[:, :, j * 512:(j + 1) * 512], 8, 512))
        for j in range(8):
            blk_src.append((w_up.rearrange("(kc p) n -> p kc n", p=128)[:, :, j * 512:(j + 1) * 512], 8, 512))
        for j in range(8):
            blk_src.append((w_down.rearrange("(kc p) n -> p kc n", p=128)[:, :, j * 128:(j + 1) * 128], 32, 128))

        scr_done = []
        with contextlib.ExitStack() as pstack:
            psb = lambda name, shape, ty=F32: pstack.enter_context(nc.sbuf_tensor("sb_" + name, shape, ty))
            NS = 4
            stg_f = [psb(f"stg_f{i}", [128, 4096]) for i in range(NS)]
            stg_b = [psb(f"stg_b{i}", [128, 4096], BF16) for i in range(NS)]
            wp_f = psb("wp_f", [128, 4, 128])
            b_stg_f = [Buf() for _ in range(NS)]
            b_stg_b = [Buf() for _ in range(NS)]
            s_ld = [Sem(nc, stack, f"d_pld{i}", 16) for i in range(NS)]
            s_st = [Sem(nc, stack, f"d_pst{i}", 16) for i in range(NS)]
            b_wpf = Buf()
            sp.op(lambda e: e.dma_start(out=wp_f[:], in_=w_pool.rearrange("g c d -> c g d")), writes=[b_wpf], sem=misc_sem())
            dve.op(lambda e: e.tensor_copy(out=wpool_bf[:], in_=wp_f[:]), reads=[b_wpf], writes=[b_wpool])

            def pload(b):
                s_ = b % NS
                src, a, n = blk_src[b]
                sp.op(lambda e: e.dma_start(out=stg_f[s_][:].rearrange("p (a n) -> p a n", a=a), in_=src),
                      writes=[b_stg_f[s_]], sem=s_ld[s_])

            NP1 = 5
            for b in range(min(NS, NP1)):
                pload(b)
            for b in range(NP1):
                s_ = b % NS
                act.op(lambda e: e.copy(out=stg_b[s_][:, 0:2048], in_=stg_f[s_][:, 0:2048]),
                       reads=[b_stg_f[s_]], writes=[b_stg_b[s_]])
                if b >= NS:
                    dve.wait_tok(scr_done[b - NS])
                t1 = dve.op(lambda e: e.tensor_copy(out=stg_b[s_][:, 2048:4096], in_=stg_f[s_][:, 2048:4096]),
                            reads=[b_stg_f[s_]])
                act.wait_tok(t1)
                tok = act.op(lambda e: e.dma_start(out=wscr[b], in_=stg_b[s_][:]), reads=[b_stg_b[s_]], sem=s_st[s_])
                scr_done.append(tok)
                if b + NS < NP1:
                    pload(b + NS)
            for q in (act, dve, pool, pe):
                for tok in scr_done[-NS:]:
                    q.wait_tok(tok)
            for tok in scr_done:
                sp.wait_tok(tok)
        scr_tok = {b: scr_done[b] for b in range(NP1)}

        wring = [sb(f"wring{i}", [128, 4096], BF16) for i in range(NB)]
        b_wring = [Buf(f"wring{i}") for i in range(NB)]
        s_wring = [Sem(nc, stack, f"d_w{i}", 16) for i in range(NB)]
        NXT = 4
        xtok = [sb(f"xtok{i}", [128, D]) for i in range(NXT)]
        b_xtok = [Buf() for _ in range(NXT)]
        s_xtok = [Sem(nc, stack, f"d_x{i}", 16) for i in range(NXT)]
        ytok = [sb(f"ytok{i}", [128, D]) for i in range(2)]
        b_ytok = [Buf(), Buf()]
        s_ytok = [Sem(nc, stack, f"d_y{i}", 16) for i in range(2)]
        xT = sb("xT", [128, 8, TT], BF16)
        b_xT = [Buf(f"xT{k}") for k in range(8)]
        mix = sb("mix", [128, 8, TT], BF16)
        b_mix = [Buf(f"mix{k}") for k in range(8)]
        hb = sb("hb", [128, 8, TT], BF16)
        b_hb = [Buf(f"hb{k}") for k in range(8)]
        XB = [sb(f"X{i}", [128, 8, TT]) for i in range(2)]
        b_XB = [[Buf(f"X{i}_{k}") for k in range(8)] for i in range(2)]
        rb = sb("rb", [128, 32, TT], BF16)
        b_rb = [Buf(f"rb{k}") for k in range(32)]
        uext = sb("uext", [128, 4, PH + TT])
        b_u = [Buf(f"u{k}") for k in range(4)]
        vext = sb("vext", [128, 4, CH + TT])
        b_v = [Buf(f"v{k}") for k in range(4)]
        acc = sb("acc", [128, 4, TT])
        b_acc = [Buf(f"acc{k}") for k in range(4)]
        dbf = sb("dbf", [128, 4, TT], BF16)
        b_d = [Buf(f"d{k}") for k in range(4)]
        ptmp = [sb(f"ptmp{i}", [128, PH + TT]) for i in range(2)]
        b_ptmp = [Buf(), Buf()]
        tmps = [sb(f"tmp{i}", [128, TT]) for i in range(NTMP)]
        b_tmps = [Buf(f"tmp{i}") for i in range(NTMP)]
        tmp_rr = [0]
        NRT = 2
        rtmps = [sb(f"rtmp{i}", [128, TT]) for i in range(NRT)]
        b_rtmps = [Buf(f"rtmp{i}") for i in range(NRT)]
        rtmp_rr = [0]
        NBT = 6
        btmps = [sb(f"btmp{i}", [128, TT], BF16) for i in range(NBT)]
        b_btmps = [Buf() for _ in range(NBT)]
        btmp_live = [False] * NBT
        btmp_rr = [0]
        _lnA = [sb(f"lnA{i}", [128, TT]) for i in range(1)]
        _lnB = [sb(f"lnB{i}", [128, TT]) for i in range(1)]
        _b_ln = [Buf(f"ln{i}") for i in range(1)]
        lnA = [_lnA[0], _lnA[0]]
        lnB = [_lnB[0], _lnB[0]]
        b_ln = [_b_ln[0], _b_ln[0]]
        hist_io = sb("hist_io", [32, 512])
        b_hist_io = Buf()
        print("SBUF bytes remaining per partition:", nc.sbuf_bytes_remaining)

        RING = [0, 1, 2, 3, 6, 7]

        def next_bank4():
            i = RING[bank_rr[0] % len(RING)]
            bank_rr[0] += 1
            return banks[i], bank_buf[i]

        def tmp():
            i = tmp_rr[0] % NTMP
            tmp_rr[0] += 1
            return tmps[i], b_tmps[i]

        def rtmp():
            i = rtmp_rr[0] % NRT
            rtmp_rr[0] += 1
            return rtmps[i], b_rtmps[i]

        def btmp():
            for _ in range(NBT):
                i = btmp_rr[0] % NBT
                btmp_rr[0] += 1
                if not btmp_live[i]:
                    break
            else:
                raise AssertionError("all bf16 temps hold chunks whose consumer is not emitted yet")
            btmp_live[i] = True
            return btmps[i], b_btmps[i], i

        class Stream:
            def __init__(self):
                self.order = []
                self.issued = 0
                self.opened = 0
                self.open = {}
                self.closed = set()
                self.closed_upto = 0

            def _issue_upto(self, n):
                while self.issued < min(n, len(self.order)):
                    j = self.issued
                    s_ = j % NB
                    blk = self.order[j]
                    sp.wait_tok(scr_tok[blk])
                    sp.op(lambda e: e.dma_start(out=wring[s_][:], in_=wscr[blk]), writes=[b_wring[s_]], sem=s_wring[s_])
                    self.issued += 1

            def use(self, blk):
                if blk not in self.open:
                    j = self.opened
                    assert self.order[j] == blk, (j, self.order[j], blk)
                    self._issue_upto(max(self.issued, min(self.closed_upto + NB, j + 1)))
                    assert j < self.issued, "weight ring too shallow for this consumption pattern"
                    self.open[blk] = j
                    self.opened += 1
                j = self.open[blk]
                return wring[j % NB], b_wring[j % NB]

            def close(self, blk):
                j = self.open.pop(blk)
                self.closed.add(j)
                while self.closed_upto in self.closed:
                    self.closed.discard(self.closed_upto)
                    self.closed_upto += 1
                self._issue_upto(self.closed_upto + NB)

        stream = Stream()

        def mm_group(out_ap, out_buf, items):
            n = len(items)
            rbufs = []
            for i, (l, r, bufs) in enumerate(items):
                if i == 0:
                    pe.deps(reads=bufs, writes=[out_buf])
                else:
                    pe.deps(reads=bufs)
                ins = pe.eng.matmul(out_ap, l, r, start=(i == 0), stop=(i == n - 1))
                for b in bufs:
                    if b not in rbufs:
                        rbufs.append(b)
            return pe.commit(ins, reads=rbufs, writes=[out_buf])

        class LNAcc:
            def __init__(self, which, nchunks, T):
                self.which, self.n, self.T = which, nchunks, T
                bi = 6 if which == 2 else 4
                self.s1, self.b1 = banks[bi], bank_buf[bi]
                self.s2, self.b2 = banks[bi + 1], bank_buf[bi + 1]
                self.k = 0
                self.pend = []

            def prep(self, src_ap, src_buf):
                T = self.T
                cb, bcb, ic = btmp()
                qb, bqb, iq = btmp()
                act.op(lambda e: e.copy(out=cb[:, :T], in_=src_ap), reads=[src_buf], writes=[bcb])
                act.op(lambda e: e.activation(out=qb[:, :T], in_=src_ap, func=AF.Square), reads=[src_buf], writes=[bqb])
                self.pend.append((cb, bcb, qb, bqb, ic, iq))

            def add_pending(self, keep=0):
                while len(self.pend) > keep:
                    self._add(self.pend.pop(0))

            def _add(self, ops):
                cb, bcb, qb, bqb, ic, iq = ops
                T = self.T
                k = self.k
                first, last = (k == 0), (k == self.n - 1)
                pe.deps(reads=[bcb] + CONST, writes=[self.b1] if first else [])
                i1 = pe.eng.matmul(self.s1[:, :T], ones_bf[:], cb[:, :T], start=first, stop=last)
                pe.commit(i1, reads=[bcb], writes=[self.b1] if last else [])
                pe.deps(reads=[bqb], writes=[self.b2] if first else [])
                i2 = pe.eng.matmul(self.s2[:, :T], ones_bf[:], qb[:, :T], start=first, stop=last)
                pe.commit(i2, reads=[bqb], writes=[self.b2] if last else [])
                btmp_live[ic] = False
                btmp_live[iq] = False
                self.k += 1

            def finish(self):
                self.add_pending()
                assert self.k == self.n
                T = self.T
                nf = float(self.n * 128)
                A, Bc, bl = lnA[self.which], lnB[self.which], b_ln[self.which]
                q, bq = tmp()
                act.op(lambda e: e.activation(out=q[:, :T], in_=self.s1[:, :T], func=AF.Square, scale=1.0 / nf),
                       reads=[self.b1], writes=[bq])
                dve.op(lambda e: e.scalar_tensor_tensor(out=q[:, :T], in0=self.s2[:, :T], scalar=1.0 / nf, in1=q[:, :T],
                                                        op0=ALU.mult, op1=ALU.subtract), reads=[self.b2, bq], writes=[bq])
                l_, bl_ = tmp()
                act.op(lambda e: e.activation(out=l_[:, :T], in_=q[:, :T], func=AF.Ln, bias=epsc[:, 0:1], scale=1.0),
                       reads=[bq, b_ones], writes=[bl_])
                act.op(lambda e: e.activation(out=A[:, :T], in_=l_[:, :T], func=AF.Exp, scale=-0.5),
                       reads=[bl_], writes=[bl])
                dve.op(lambda e: e.scalar_tensor_tensor(out=q[:, :T], in0=q[:, :T], scalar=EPS, in1=A[:, :T],
                                                        op0=ALU.add, op1=ALU.mult), reads=[bq, bl], writes=[bq])
                dve.op(lambda e: e.tensor_tensor(out=q[:, :T], in0=q[:, :T], in1=A[:, :T], op=ALU.mult),
                       reads=[bq, bl], writes=[bq])
                dve.op(lambda e: e.scalar_tensor_tensor(out=q[:, :T], in0=q[:, :T], scalar=-0.5, in1=A[:, :T],
                                                        op0=ALU.mult, op1=ALU.mult), reads=[bq, bl], writes=[bq])
                dve.op(lambda e: e.scalar_tensor_tensor(out=A[:, :T], in0=A[:, :T], scalar=1.5, in1=q[:, :T],
                                                        op0=ALU.mult, op1=ALU.add), reads=[bq, bl], writes=[bl])
                dve.op(lambda e: e.scalar_tensor_tensor(out=Bc[:, :T], in0=self.s1[:, :T], scalar=-1.0 / nf, in1=A[:, :T],
                                                        op0=ALU.mult, op1=ALU.mult), reads=[self.b1, bl], writes=[bl])

        def ln_apply(which, src_ap, src_buf, T):
            A, Bc, bl = lnA[which], lnB[which], b_ln[which]
            t1, bt1 = tmp()
            dve.op(lambda e: e.tensor_tensor(out=t1[:, :T], in0=src_ap, in1=A[:, :T], op=ALU.mult),
                   reads=[src_buf, bl], writes=[bt1])
            dve.op(lambda e: e.tensor_tensor(out=t1[:, :T], in0=t1[:, :T], in1=Bc[:, :T], op=ALU.add),
                   reads=[bt1, bl], writes=[bt1])
            return t1, bt1

        def hist_zero():
            for c in range(4):
                pool.op(lambda e: e.memset(uext[:, c, 0:PH], 0.0), writes=[b_u[c]])
                pool.op(lambda e: e.memset(vext[:, c, 0:CH], 0.0), writes=[b_v[c]])

        def hist_load(src, n, dst, bdst):
            sp.op(lambda e: e.dma_start(out=hist_io[0:n, :], in_=src), writes=[b_hist_io], sem=misc_sem())
            bk, bb = next_bank4()
            pe.deps(reads=[b_hist_io] + CONST, writes=[bb])
            for c in range(4):
                ins = pe.eng.transpose(bk[:, c * 32:c * 32 + n], hist_io[0:n, c * 128:(c + 1) * 128], ident[0:n, 0:n])
            pe.commit(ins, reads=[b_hist_io], writes=[bb])
            for c in range(4):
                act.op(lambda e: e.copy(out=dst[:, c, 0:n], in_=bk[:, c * 32:c * 32 + n]), reads=[bb], writes=[bdst[c]])

        def hist_store(srct, bsrc, T, n, dst_dram):
            bk, bb = next_bank4()
            pe.deps(reads=list(bsrc) + CONST, writes=[bb])
            for c in range(4):
                ins = pe.eng.transpose(bk[0:n, c * 128:(c + 1) * 128], srct[:, c, T:T + n], ident[:, :])
            pe.commit(ins, reads=list(bsrc), writes=[bb])
            act.op(lambda e: e.copy(out=hist_io[0:n, :], in_=bk[0:n, :]), reads=[bb], writes=[b_hist_io])
            return sp.op(lambda e: e.dma_start(out=dst_dram, in_=hist_io[0:n, :]), reads=[b_hist_io], sem=misc_sem())

        sf_all = rb[:].rearrange("p a n -> p (a n)").bitcast(F32)
        sbf_all = XB[1][:].rearrange("p a n -> p (a n)").bitcast(BF16)
        SF = [sf_all[:, i * 4096:(i + 1) * 4096] for i in range(2)]
        SBF = [sbf_all[:, i * 4096:(i + 1) * 4096] for i in range(2)]
        b_SF = [Buf(), Buf()]
        b_SBF = [Buf(), Buf()]
        s_ld2 = [Sem(nc, stack, f"d_p2ld{i}", 16) for i in range(2)]
        s_st2 = [Sem(nc, stack, f"d_p2st{i}", 16) for i in range(2)]

        def p2_load(k):
            b = NP1 + k
            if b >= NBLK:
                return
            src, a, n = blk_src[b]
            sp.op(lambda e: e.dma_start(out=SF[k % 2].rearrange("p (a n) -> p a n", a=a), in_=src),
                  writes=[b_SF[k % 2]], sem=s_ld2[k % 2])

        def p2_step(k):
            b = NP1 + k
            if b >= NBLK:
                return
            act.op(lambda e: e.copy(out=SBF[k % 2], in_=SF[k % 2]), reads=[b_SF[k % 2]], writes=[b_SBF[k % 2]])
            scr_tok[b] = act.op(lambda e: e.dma_start(out=wscr[b], in_=SBF[k % 2]), reads=[b_SBF[k % 2]], sem=s_st2[k % 2])
            p2_load(k + 2)

        def p2_barrier():
            for b in range(NBLK - 2, NBLK):
                for q in (act, dve, pe, pool):
                    q.wait_tok(scr_tok[b])

        xcnt = [0]
        ycnt = [0]
        final_toks = []

        class Tile:
            def __init__(self, gi, x_dram, y_dram, T, first):
                self.gi, self.x_dram, self.y_dram, self.T, self.first = gi, x_dram, y_dram, T, first
                self.X = XB[gi % 2]
                self.bX = b_XB[gi % 2]
                self.nchunk = max(1, T // 128)
                self.rows = min(T, 128)
                self.ln1 = None
                self.ln2 = None
                self.conv_pos = 0
                self.xslots = []

            def xload(self):
                self.xslots = []
                for tc in range(self.nchunk):
                    xs_ = xcnt[0] % NXT
                    xcnt[0] += 1
                    self.xslots.append(xs_)
                    sp.op(lambda e: e.dma_start(out=xtok[xs_][0:self.rows, :], in_=self.x_dram[tc * 128:tc * 128 + self.rows, :]),
                          writes=[b_xtok[xs_]], sem=s_xtok[xs_])

            def A1(self):
                self.A1a()
                self.A1b()
                self.A1c()

            def A1a(self):
                T, rows, X, bX = self.T, self.rows, self.X, self.bX
                for tc in range(self.nchunk):
                    xs_ = self.xslots[tc]
                    for half in range(2):
                        bk, bb = next_bank4()
                        pe.deps(reads=[b_xtok[xs_]] + CONST, writes=[bb])
                        for j in range(4):
                            fc = half * 4 + j
                            ins = pe.eng.transpose(bk[:, j * 128:j * 128 + rows],
                                                   xtok[xs_][0:rows, fc * 128:(fc + 1) * 128], ident[0:rows, 0:rows])
                        pe.commit(ins, reads=[b_xtok[xs_]], writes=[bb])
                        src = bk[:].rearrange("p (j t) -> p j t", j=4)[:, :, 0:rows]
                        act.op(lambda e: e.copy(out=xT[:, half * 4:half * 4 + 4, tc * 128:tc * 128 + rows], in_=src),
                               reads=[bb], writes=b_xT[half * 4:half * 4 + 4])
                        act.op(lambda e: e.activation(out=X[:, half * 4:half * 4 + 4, tc * 128:tc * 128 + rows], in_=src,
                                                      func=AF.Identity, bias=0.0, scale=ALPHA),
                               reads=[bb], writes=bX[half * 4:half * 4 + 4])

            def A1b(self):
                T, rows, X, bX = self.T, self.rows, self.X, self.bX

                def inproj(mc):
                    blk, bblk = stream.use(mc // 4)
                    w3 = blk[:].rearrange("p (a n) -> p a n", a=8)
                    bk, bb = next_bank4()
                    items = [(w3[:, kc, (mc % 4) * 128:(mc % 4 + 1) * 128], xT[:, kc, 0:T], [bblk, b_xT[kc]])
                             for kc in range(8)]
                    mm_group(bk[:, :T], bb, items)
                    return bk, bb

                for c in range(4):
                    bkg, bbg = inproj(8 + c)
                    sg, bsg = tmp()
                    act.op(lambda e: e.activation(out=sg[:, :T], in_=bkg[:, :T], func=AF.Sigmoid,
                                                  bias=pcol(P_BIN, 8 + c), scale=1.0), reads=[bbg] + CONST, writes=[bsg])
                    bka, bba = inproj(4 + c)
                    dve.op(lambda e: e.scalar_tensor_tensor(out=vext[:, c, CH:CH + T], in0=bka[:, :T], scalar=pcol(P_BIN, 4 + c),
                                                            in1=sg[:, :T], op0=ALU.add, op1=ALU.mult),
                           reads=[bba, bsg] + CONST, writes=[b_v[c]])
                stream.close(2)
                stream.close(1)
                for c in range(4):
                    bk, bb = inproj(c)
                    act.op(lambda e: e.activation(out=uext[:, c, PH:PH + T], in_=bk[:, :T], func=AF.Identity,
                                                  bias=pcol(P_BIN, c), scale=1.0), reads=[bb] + CONST, writes=[b_u[c]])
                stream.close(0)

                for g, w in enumerate(WINS):
                    L = PH + T
                    cur, bcur = uext[:, g, :], b_u[g]
                    sh = 1
                    pi = 0
                    while sh < w:
                        dst, bdst = ptmp[pi % 2], b_ptmp[pi % 2]
                        pi += 1
                        lo = 2 * sh - 1
                        src_ap = cur
                        dve.op(lambda e: e.tensor_tensor(out=dst[:, lo:L], in0=src_ap[:, lo:L], in1=src_ap[:, lo - sh:L - sh],
                                                         op=ALU.add), reads=[bcur], writes=[bdst])
                        cur, bcur = dst, bdst
                        sh *= 2
                    sw = cur
                    dve.op(lambda e: e.scalar_tensor_tensor(out=dbf[:, g, 0:T], in0=sw[:, PH:PH + T], scalar=1.0 / w,
                                                            in1=uext[:, g, PH:PH + T], op0=ALU.mult, op1=ALU.subtract),
                           reads=[bcur, b_u[g]], writes=[b_d[g]])
                    if self.first:
                        nfix = w - 1
                        t_, bt_ = tmp()
                        dve.op(lambda e: e.tensor_tensor(out=t_[:, 0:nfix], in0=sw[:, PH:PH + nfix],
                                                         in1=rcnt[:, g * 16:g * 16 + nfix], op=ALU.mult),
                               reads=[bcur] + CONST, writes=[bt_])
                        dve.op(lambda e: e.tensor_tensor(out=dbf[:, g, 0:nfix], in0=t_[:, 0:nfix],
                                                         in1=uext[:, g, PH:PH + nfix], op=ALU.subtract),
                               reads=[bt_, b_u[g]], writes=[b_d[g]])
            def A1c(self):
                T = self.T
                for g in range(4):
                    bk, bb = next_bank4()
                    mm_group(bk[:, :T], bb, [(wpool_bf[:, g, :], dbf[:, g, 0:T], [b_d[g]] + CONST)])
                    act.op(lambda e: e.activation(out=mix[:, g, 0:T], in_=bk[:, :T], func=AF.Identity,
                                                  bias=0.0, scale=pcol(P_PSC, g)), reads=[bb] + CONST, writes=[b_mix[g]])

            def conv(self, n):
                T = self.T
                for _ in range(n):
                    if self.conv_pos >= 4 * KC:
                        return
                    k, c = divmod(self.conv_pos, 4)
                    self.conv_pos += 1
                    wk = params[:, P_CW + c * KC + k:P_CW + c * KC + k + 1]
                    if k == 0:
                        dve.op(lambda e: e.tensor_scalar(out=acc[:, c, 0:T], in0=vext[:, c, 0:T], scalar1=wk,
                                                         scalar2=pcol(P_CB, c), op0=ALU.mult, op1=ALU.add),
                               reads=[b_v[c]] + CONST, writes=[b_acc[c]])
                    else:
                        dve.op(lambda e: e.scalar_tensor_tensor(out=acc[:, c, 0:T], in0=vext[:, c, k:k + T], scalar=wk,
                                                                in1=acc[:, c, 0:T], op0=ALU.mult, op1=ALU.add),
                               reads=[b_v[c], b_acc[c]] + CONST, writes=[b_acc[c]])

            def A2(self):
                T = self.T
                ln = LNAcc(0, 4, T)
                for c in range(4):
                    ln.prep(acc[:, c, 0:T], b_acc[c])
                    ln.add_pending()
                ln.finish()
                for c in range(4):
                    t2, bt2 = ln_apply(0, acc[:, c, 0:T], b_acc[c], T)
                    yv, byv = tmp()
                    sg, bsg = tmp()
                    act.op(lambda e: e.activation(out=yv[:, :T], in_=t2[:, :T], func=AF.Identity,
                                                  bias=pcol(P_CLB, c), scale=pcol(P_CLG, c)), reads=[bt2] + CONST, writes=[byv])
                    act.op(lambda e: e.activation(out=sg[:, :T], in_=t2[:, :T], func=AF.Sigmoid,
                                                  bias=pcol(P_CLB, c), scale=pcol(P_CLG, c)), reads=[bt2] + CONST, writes=[bsg])
                    dve.op(lambda e: e.tensor_tensor(out=mix[:, 4 + c, 0:T], in0=yv[:, :T], in1=sg[:, :T], op=ALU.mult),
                           reads=[byv, bsg], writes=[b_mix[4 + c]])
                if T == TT:
                    for c in range(4):
                        act.op(lambda e: e.copy(out=uext[:, c, 0:PH], in_=uext[:, c, T:T + PH]), reads=[b_u[c]], writes=[b_u[c]])
                        act.op(lambda e: e.copy(out=vext[:, c, 0:CH], in_=vext[:, c, T:T + CH]), reads=[b_v[c]], writes=[b_v[c]])

            def A3(self):
                for mc in range(8):
                    self.A3g(mc)

            def A3g(self, mc):
                T, X, bX = self.T, self.X, self.bX
                if mc == 0:
                    self.ln1 = LNAcc(1, 8, T)
                ln = self.ln1
                if True:
                    blk, bblk = stream.use(3 + mc // 4)
                    w3 = blk[:].rearrange("p (a n) -> p a n", a=8)
                    bk, bb = next_bank4()
                    items = [(w3[:, kc, (mc % 4) * 128:(mc % 4 + 1) * 128], mix[:, kc, 0:T], [bblk, b_mix[kc]])
                             for kc in range(8)]
                    mm_group(bk[:, :T], bb, items)
                    dve.op(lambda e: e.scalar_tensor_tensor(out=X[:, mc, 0:T], in0=bk[:, :T], scalar=pcol(P_BOUT, mc),
                                                            in1=X[:, mc, 0:T], op0=ALU.add, op1=ALU.add),
                           reads=[bb, bX[mc]] + CONST, writes=[bX[mc]])
                    ln.prep(X[:, mc, 0:T], bX[mc])
                    ln.add_pending(keep=2)
                    if mc % 4 == 3:
                        stream.close(3 + mc // 4)
                if mc == 7:
                    ln.add_pending()

            def A4(self):
                self.A4fin()
                self.A4norm()

            def A4fin(self):
                self.ln1.finish()

            def A4norm(self):
                T, X, bX = self.T, self.X, self.bX
                for mc in range(8):
                    t2, bt2 = ln_apply(1, X[:, mc, 0:T], bX[mc], T)
                    act.op(lambda e: e.activation(out=hb[:, mc, 0:T], in_=t2[:, :T], func=AF.Identity,
                                                  bias=pcol(P_B1, mc), scale=pcol(P_G1, mc)), reads=[bt2] + CONST, writes=[b_hb[mc]])
                    act.op(lambda e: e.activation(out=X[:, mc, 0:T], in_=t2[:, :T], func=AF.Identity,
                                                  bias=dpar[:, 8 + mc:9 + mc], scale=dpar[:, mc:mc + 1]),
                           reads=[bt2] + CONST, writes=[bX[mc]])

            def up(self, mc):
                T = self.T
                blk, bblk = stream.use(5 + mc // 4)
                w3 = blk[:].rearrange("p (a n) -> p a n", a=8)
                bk, bb = next_bank4()
                items = [(w3[:, kc, (mc % 4) * 128:(mc % 4 + 1) * 128], hb[:, kc, 0:T], [bblk, b_hb[kc]])
                         for kc in range(8)]
                mm_group(bk[:, :T], bb, items)
                r1, br1 = rtmp()
                act.op(lambda e: e.activation(out=r1[:, :T], in_=bk[:, :T], func=AF.Relu), reads=[bb], writes=[br1])
                act.op(lambda e: e.activation(out=rb[:, mc, 0:T], in_=r1[:, :T], func=AF.Square),
                       reads=[br1], writes=[b_rb[mc]])
                if mc % 4 == 3:
                    stream.close(5 + mc // 4)

            def down(self, mc):
                T, X, bX = self.T, self.X, self.bX
                blk, bblk = stream.use(13 + mc)
                w3 = blk[:].rearrange("p (a n) -> p a n", a=32)
                bk, bb = next_bank4()
                items = [(w3[:, kc, :], rb[:, kc, 0:T], [bblk, b_rb[kc]]) for kc in range(32)]
                mm_group(bk[:, :T], bb, items)
                stream.close(13 + mc)
                dve.op(lambda e: e.tensor_tensor(out=X[:, mc, 0:T], in0=bk[:, :T], in1=X[:, mc, 0:T], op=ALU.add),
                       reads=[bb, bX[mc]], writes=[bX[mc]])

            def Fend(self):
                for tc in range(self.nchunk):
                    self.Fout(tc)

            def Fout(self, tc):
                T, X, bX, rows = self.T, self.X, self.bX, self.rows
                ys_ = ycnt[0] % 2
                ycnt[0] += 1
                st = ystat[:, ys_, :]
                bst = b_ystat[ys_]
                hb_ = []
                for half in range(2):
                    bk, bb = next_bank4()
                    pe.deps(reads=bX[half * 4:half * 4 + 4] + CONST, writes=[bb])
                    for j in range(4):
                        fc = half * 4 + j
                        ins = pe.eng.transpose(bk[0:rows, j * 128:(j + 1) * 128], X[:, fc, tc * 128:tc * 128 + rows], ident[:, :])
                    pe.commit(ins, reads=bX[half * 4:half * 4 + 4], writes=[bb])
                    dve.op(lambda e: e.bn_stats(out=st[0:rows, half * 6:half * 6 + 6], in_=bk[0:rows, :]),
                           reads=[bb], writes=[bst])
                    hb_.append((bk, bb))
                dve.op(lambda e: e.bn_aggr(out=st[0:rows, 12:14], in_=st[0:rows, 0:12]), reads=[bst], writes=[bst])
                act.op(lambda e: e.activation(out=st[0:rows, 14:15], in_=st[0:rows, 13:14], func=AF.Ln,
                                              bias=epsc[0:rows, 0:1], scale=1.0), reads=[bst, b_ones], writes=[bst])
                act.op(lambda e: e.activation(out=st[0:rows, 15:16], in_=st[0:rows, 14:15], func=AF.Exp, scale=-0.5),
                       reads=[bst], writes=[bst])
                dve.op(lambda e: e.scalar_tensor_tensor(out=st[0:rows, 17:18], in0=st[0:rows, 13:14], scalar=EPS,
                                                        in1=st[0:rows, 15:16], op0=ALU.add, op1=ALU.mult),
                       reads=[bst], writes=[bst])
                dve.op(lambda e: e.tensor_tensor(out=st[0:rows, 17:18], in0=st[0:rows, 17:18], in1=st[0:rows, 15:16],
                                                 op=ALU.mult), reads=[bst], writes=[bst])
                dve.op(lambda e: e.scalar_tensor_tensor(out=st[0:rows, 17:18], in0=st[0:rows, 17:18], scalar=-0.5,
                                                        in1=st[0:rows, 15:16], op0=ALU.mult, op1=ALU.mult),
                       reads=[bst], writes=[bst])
                dve.op(lambda e: e.scalar_tensor_tensor(out=st[0:rows, 15:16], in0=st[0:rows, 15:16], scalar=1.5,
                                                        in1=st[0:rows, 17:18], op0=ALU.mult, op1=ALU.add),
                       reads=[bst], writes=[bst])
                dve.op(lambda e: e.scalar_tensor_tensor(out=st[0:rows, 16:17], in0=st[0:rows, 12:13], scalar=-1.0,
                                                        in1=st[0:rows, 15:16], op0=ALU.mult, op1=ALU.mult),
                       reads=[bst], writes=[bst])
                for half, (bk, bb) in enumerate(hb_):
                    act.op(lambda e: e.activation(out=ytok[ys_][0:rows, half * 512:(half + 1) * 512], in_=bk[0:rows, :],
                                                  func=AF.Identity, bias=st[0:rows, 16:17], scale=st[0:rows, 15:16]),
                           reads=[bb, bst], writes=[b_ytok[ys_]])
                dve.op(lambda e: e.tensor_tensor(out=ytok[ys_][0:rows, :], in0=ytok[ys_][0:rows, :], in1=g2bc[0:rows, :],
                                                 op=ALU.mult), reads=[b_ytok[ys_]] + CONST, writes=[b_ytok[ys_]])
                dve.op(lambda e: e.tensor_tensor(out=ytok[ys_][0:rows, :], in0=ytok[ys_][0:rows, :], in1=b2bc[0:rows, :],
                                                 op=ALU.add), reads=[b_ytok[ys_]] + CONST, writes=[b_ytok[ys_]])
                tok = act.op(lambda e: e.dma_start(out=self.y_dram[tc * 128:tc * 128 + rows, :], in_=ytok[ys_][0:rows, :]),
                             reads=[b_ytok[ys_]], sem=s_ytok[ys_])
                final_toks.append(tok)

        tiles = [Tile(i, xp[i * TT:(i + 1) * TT, :], yp[i * TT:(i + 1) * TT, :], TT, first=(i == 0)) for i in range(n_ptiles)]
        stile = Tile(n_ptiles, xs, ys, 16, first=False)
        items = []
        A_BLK = [2, 1, 0]
        items.append(("hist_zero",))
        items.append(("xload", tiles[0]))
        items.append(("prefetch",))
        items.append(("p2_load", 0))
        items.append(("p2_load", 1))
        items.append(("A1a", tiles[0]))
        items.append(("p2_step", 0))
        items.append(("p2_step", 1))
        items.append(("A1b", tiles[0]))
        items.append(("p2_step", 2))
        items.append(("p2_step", 3))
        items.append(("A1c", tiles[0]))
        tiles.append(stile)
        items.append(("xload", tiles[1]))
        for k in range(8):
            items.append(("p2_step", 4 + k))
            items.append(("conv", tiles[0], 16))
        items.append(("A2", tiles[0]))
        items.append(("p2_step", 12))
        items.append(("p2_step", 13))
        for mc in range(8):
            items.append(("A3g", tiles[0], mc))
            if mc == 1:
                items.append(("p2_step", 14))
            if mc == 4:
                items.append(("p2_step", 15))
        items.append(("A4", tiles[0]))
        items.append(("p2_barrier",))
        for i in range(1, n_ptiles + 1):
            t, p = tiles[i], tiles[i - 1]
            nxt = tiles[i + 1] if i + 1 <= n_ptiles else None
            if i == 1:
                items.append(("A1a", t))
            if t is stile:
                items.append(("hist_swap",))
            items.append(("A1b", t))
            if nxt is not None:
                items.append(("xload", nxt))
            for k in range(32):
                items.append(("up", p, k))
                items.append(("conv", t, 3))
                if k == 5:
                    items.append(("A1c", t))
            for k in range(5):
                items.append(("down", p, k))
                items.append(("conv", t, 6))
            items.append(("A2", t))
            for k in range(5, 8):
                items.append(("down", p, k))
            for mc in range(8):
                items.append(("A3g", t, mc))
                if mc % 2 == 1 and mc // 2 < p.nchunk:
                    items.append(("Fout", p, mc // 2))
            items.append(("A4fin", t))
            if nxt is not None:
                items.append(("A1a", nxt))
            items.append(("A4norm", t))
        p = tiles[-1]
        items += [("up", p, k) for k in range(32)] + [("down", p, k) for k in range(8)] + [("Fend", p)]
        items.append(("hist_final",))

        for it in items:
            if it[0] in ("A1", "A1b"):
                stream.order += A_BLK
            elif it[0] == "A3":
                stream.order += [3, 4]
            elif it[0] == "A3g" and it[2] % 4 == 0:
                stream.order.append(3 + it[2] // 4)
            elif it[0] == "up" and it[2] % 4 == 0:
                stream.order.append(5 + it[2] // 4)
            elif it[0] == "down":
                stream.order.append(13 + it[2])

        for it in items:
            kind = it[0]
            if kind == "hist_zero":
                hist_zero()
            elif kind == "prefetch":
                stream._issue_upto(NB)
            elif kind == "p2_load":
                p2_load(it[1])
            elif kind == "p2_step":
                p2_step(it[1])
            elif kind == "p2_barrier":
                p2_barrier()
            elif kind == "hist_swap":
                final_toks.append(hist_store(uext, b_u, TT, PH, o_pp))
                final_toks.append(hist_store(vext, b_v, TT, CH, o_cp))
                hist_load(sp_in, PH, uext, b_u)
                hist_load(sc_in, CH, vext, b_v)
            elif kind == "hist_final":
                final_toks.append(hist_store(uext, b_u, 16, PH, o_ps))
                final_toks.append(hist_store(vext, b_v, 16, CH, o_cs))
            elif kind in ("conv", "up", "down", "A3g", "Fout"):
                getattr(it[1], kind)(it[2])
            else:
                getattr(it[1], kind)()
        assert stream.opened == len(stream.order) and not stream.open

        for tok in final_toks:
            sp.wait_tok(tok)
        for q in (pe, act, dve, pool):
            if q.sem.n:
                sp.wait_tok((q.sem, q.sem.n))
    return nc


def _pack_params(b_in, pool_scale, conv_w, conv_b, conv_ln_g, conv_ln_b, b_out, ln1_g, ln1_b, ln2_g, ln2_b):
    def cols(v):
        v = np.asarray(v, np.float32).reshape(-1, 128)
        return v.T
    cw = np.asarray(conv_w, np.float32).reshape(KC, 4, 128)
    cw = np.transpose(cw, (2, 1, 0)).reshape(128, 4 * KC)
    parts = [cols(b_in), cols(pool_scale), cols(conv_b), cols(conv_ln_g), cols(conv_ln_b), cols(b_out),
             cols(ln1_g), cols(ln1_b), cols(ln2_g), cols(ln2_b), cw]
    out = np.ascontiguousarray(np.concatenate(parts, axis=1), dtype=np.float32)
    assert out.shape == (128, NPAR), out.shape
    return out


def _consts():
    ident = np.eye(128, dtype=np.float32)
    rc = np.zeros((4, 16), np.float32)
    for g, w in enumerate(WINS):
        for t in range(16):
            rc[g, t] = 1.0 / min(t + 1, w)
    rcnt = np.ascontiguousarray(np.broadcast_to(rc.reshape(1, 64), (128, 64)), dtype=np.float32)
    return ident, rcnt


_NC_CACHE = {}


def run(x_prompt, x_sample, state_pool, state_conv, w_in, b_in, w_pool, pool_scale, conv_w, conv_b, conv_ln_g,
        conv_ln_b, w_out, b_out, ln1_g, ln1_b, w_up, w_down, ln2_g, ln2_b, cfg=None, trace=False):
    f = lambda a: np.ascontiguousarray(np.asarray(a, dtype=np.float32))
    x_prompt = f(x_prompt)
    nb, seq, _ = x_prompt.shape
    assert seq % TT == 0
    n_ptiles = seq // TT
    key = (n_ptiles, repr(sorted((cfg or {}).items())))
    if key not in _NC_CACHE:
        _NC_CACHE[key] = build(n_ptiles, cfg)
    nc = _NC_CACHE[key]
    params = _pack_params(b_in[0], pool_scale[0], conv_w[0], conv_b[0], conv_ln_g[0], conv_ln_b[0], b_out[0],
                          ln1_g[0], ln1_b[0], ln2_g[0], ln2_b[0])
    ident, rcnt = _consts()
    shared = {"w_in": f(w_in[0]), "w_out": f(w_out[0]), "w_up": f(w_up[0]), "w_down": f(w_down[0]),
              "w_pool": f(w_pool[0]), "params": params, "ident": ident, "rcnt": rcnt,
              "ln2g": f(ln2_g[0]).reshape(1, D), "ln2b": f(ln2_b[0]).reshape(1, D)}
    x_sample, state_pool, state_conv = f(x_sample), f(state_pool), f(state_conv)
    in_maps = []
    for c in range(8):
        m = dict(shared)
        m["xp"] = x_prompt[c]
        m["xs"] = x_sample[c]
        m["sp"] = state_pool[0, c]
        m["sc"] = state_conv[0, c]
        in_maps.append(m)
    res = run_bass_kernel_spmd(nc, in_maps, core_ids=list(range(8)), trace=trace)
    R = res.results
    stk = lambda k: np.stack([np.asarray(R[c][k], dtype=np.float32) for c in range(8)], axis=0)
    outs = (stk("yp"), stk("ys"), stk("o_pp")[None], stk("o_cp")[None], stk("o_ps")[None], stk("o_cs")[None])
    return outs, res


def kernel(**inputs):
    outs, _ = run(**inputs)
    return outs
```

```python
import contextlib
import numpy as np
import concourse.bass as bass
import concourse.mybir as mybir
from concourse.bass_utils import run_bass_kernel_spmd

F32 = mybir.dt.float32
BF16 = mybir.dt.bfloat16
ALU = mybir.AluOpType
AF = mybir.ActivationFunctionType

D = 1024
DFF = 4096
PW = 512
CW = 512
DIN = 1536
KC = 31
PH = 15
CH = 30
WINS = (2, 4, 8, 16)
ALPHA = 2.0 ** 0.25
EPS = 1e-5
TT = 512
NBLK = 21

P_BIN, P_PSC, P_CB, P_CLG, P_CLB, P_BOUT, P_G1, P_B1, P_G2, P_B2, P_CW = 0, 12, 16, 20, 24, 28, 36, 44, 52, 60, 68
NPAR = P_CW + 4 * KC


class Sem:
    def __init__(self, nc, stack, name, step):
        self.h = stack.enter_context(nc.semaphore(name))
        self.step = step
        self.n = 0

    def next(self):
        self.n += self.step
        return (self, self.n)


class Buf:
    __slots__ = ("w", "r", "name", "excl")

    def __init__(self, name="", excl=False):
        self.w = None
        self.r = {}
        self.name = name
        self.excl = excl


class Q:
    def __init__(self, nc, stack, eng, name):
        self.eng = eng
        self.name = name
        self.sem = Sem(nc, stack, "q_" + name, 1)
        self.seen = {}
        self.nwaits = 0

    def wait_tok(self, tok):
        if tok is None:
            return
        s, n = tok
        if self.seen.get(s, 0) >= n:
            return
        self.seen[s] = n
        self.eng.wait_ge(s.h, n)
        self.nwaits += 1

    def deps(self, reads=(), writes=()):
        for b in reads:
            self.wait_tok(b.w)
            if b.excl:
                for s, n in b.r.items():
                    if s is not self.sem:
                        self.wait_tok((s, n))
        for b in writes:
            self.wait_tok(b.w)
            for s, n in b.r.items():
                self.wait_tok((s, n))

    def commit(self, ins, reads=(), writes=(), sem=None):
        sem = sem or self.sem
        tok = sem.next()
        ins.then_inc(sem.h, sem.step)
        for b in reads:
            if b.r.get(tok[0], 0) < tok[1]:
                b.r[tok[0]] = tok[1]
        for b in writes:
            b.w = tok
            b.r = {}
        return tok

    def op(self, mk, reads=(), writes=(), sem=None):
        self.deps(reads, writes)
        return self.commit(mk(self.eng), reads, writes, sem)


def build(n_ptiles, cfg=None):
    cfg = dict(cfg or {})
    NB = cfg.get("nb", 3)
    NTMP = cfg.get("ntmp", 5)
    sq_on_pool_every = cfg.get("sq_dve_mod", 0)
    SEQ = n_ptiles * TT

    nc = bass.Bass("TRN2", target_bir_lowering=False)
    dt = lambda name, shape, kind, ty=F32: nc.dram_tensor(name, shape, ty, kind=kind).ap()
    xp = dt("xp", [SEQ, D], "ExternalInput")
    xs = dt("xs", [16, D], "ExternalInput")
    sp_in = dt("sp", [PH, PW], "ExternalInput")
    sc_in = dt("sc", [CH, CW], "ExternalInput")
    w_in = dt("w_in", [D, DIN], "ExternalInput")
    w_out = dt("w_out", [D, D], "ExternalInput")
    w_up = dt("w_up", [D, DFF], "ExternalInput")
    w_down = dt("w_down", [DFF, D], "ExternalInput")
    w_pool = dt("w_pool", [4, 128, 128], "ExternalInput")
    params_in = dt("params", [128, NPAR], "ExternalInput")
    ident_in = dt("ident", [128, 128], "ExternalInput")
    rcnt_in = dt("rcnt", [128, 64], "ExternalInput")
    ln2g_in = dt("ln2g", [1, D], "ExternalInput")
    ln2b_in = dt("ln2b", [1, D], "ExternalInput")
    yp = dt("yp", [SEQ, D], "ExternalOutput")
    ys = dt("ys", [16, D], "ExternalOutput")
    o_pp = dt("o_pp", [PH, PW], "ExternalOutput")
    o_cp = dt("o_cp", [CH, CW], "ExternalOutput")
    o_ps = dt("o_ps", [PH, PW], "ExternalOutput")
    o_cs = dt("o_cs", [CH, CW], "ExternalOutput")
    wscr = dt("wscr", [NBLK, 128, 4096], "Internal", BF16)

    stack = contextlib.ExitStack()
    with stack:
        sb = lambda name, shape, ty=F32: stack.enter_context(nc.sbuf_tensor("sb_" + name, shape, ty))
        pe = Q(nc, stack, nc.tensor, "pe")
        act = Q(nc, stack, nc.scalar, "act")
        dve = Q(nc, stack, nc.vector, "dve")
        pool = Q(nc, stack, nc.gpsimd, "pool")
        sp = Q(nc, stack, nc.sync, "sp")

        params = sb("params", [128, NPAR])
        ident = sb("ident", [128, 128])
        rcnt = sb("rcnt", [128, 64])
        ones_bf = sb("ones_bf", [128, 128], BF16)
        epsc = sb("epsc", [128, 8])
        dpar = sb("dpar", [128, 16])
        wpool_bf = sb("wpool_bf", [128, 4, 128], BF16)
        g2bc = sb("g2bc", [128, D])
        b2bc = sb("b2bc", [128, D])
        ystat = sb("ystat", [128, 2, 24])
        b_ystat = [Buf("ystat0"), Buf("ystat1")]
        b_params, b_ident, b_rcnt, b_ones, b_dpar, b_wpool = (Buf(n) for n in ("params", "ident", "rcnt", "ones", "dpar", "wpool"))
        b_g2bc, b_b2bc = Buf("g2bc"), Buf("b2bc")
        CONST = [b_params, b_ident, b_rcnt, b_ones, b_dpar, b_wpool, b_g2bc, b_b2bc]
        misc_n = [0]

        def misc_sem():
            misc_n[0] += 1
            return Sem(nc, stack, f"d_misc{misc_n[0]}", 16)

        def pcol(base, c):
            return params[:, base + c:base + c + 1]

        banks = [stack.enter_context(nc.psum_tensor(f"bank{i}", [128, 512], F32)) for i in range(8)]
        bank_buf = [Buf(f"bank{i}", excl=True) for i in range(8)]
        bank_rr = [0]

        def next_bank():
            i = bank_rr[0] % 6
            bank_rr[0] += 1
            return banks[i], bank_buf[i]

        sp.op(lambda e: e.dma_start(out=params[:], in_=params_in), writes=[b_params], sem=misc_sem())
        sp.op(lambda e: e.dma_start(out=ident[:], in_=ident_in), writes=[b_ident], sem=misc_sem())
        sp.op(lambda e: e.dma_start(out=rcnt[:], in_=rcnt_in), writes=[b_rcnt], sem=misc_sem())
        sp.op(lambda e: e.dma_start(out=g2bc[:], in_=ln2g_in.broadcast_to([128, D])), writes=[b_g2bc], sem=misc_sem())
        sp.op(lambda e: e.dma_start(out=b2bc[:], in_=ln2b_in.broadcast_to([128, D])), writes=[b_b2bc], sem=misc_sem())
        dve.op(lambda e: e.memset(ones_bf[:], 1.0), writes=[b_ones])
        dve.op(lambda e: e.memset(epsc[:], EPS), writes=[b_ones])
        dve.op(lambda e: e.tensor_scalar(out=dpar[:, 0:8], in0=params[:, P_G1:P_G1 + 8], scalar1=ALPHA, scalar2=None,
                                         op0=ALU.mult), reads=[b_params], writes=[b_dpar])
        dve.op(lambda e: e.tensor_scalar(out=dpar[:, 8:16], in0=params[:, P_B1:P_B1 + 8], scalar1=ALPHA, scalar2=None,
                                         op0=ALU.mult), reads=[b_params], writes=[b_dpar])

        blk_src = []
        for j in range(3):
            blk_src.append((w_in.rearrange("(kc p) n -> p kc n", p=128)[:, :, j * 512:(j + 1) * 512], 8, 512))
        for j in range(2):
            blk_src.append((w_out.rearrange("(kc p) n -> p kc n", p=128)[:, :, j * 512:(j + 1) * 512], 8, 512))
        for j in range(8):
            blk_src.append((w_up.rearrange("(kc p) n -> p kc n", p=128)[:, :, j * 512:(j + 1) * 512], 8, 512))
        for j in range(8):
            blk_src.append((w_down.rearrange("(kc p) n -> p kc n", p=128)[:, :, j * 128:(j + 1) * 128], 32, 128))

        scr_done = []
        with contextlib.ExitStack() as pstack:
            psb = lambda name, shape, ty=F32: pstack.enter_context(nc.sbuf_tensor("sb_" + name, shape, ty))
            NS = 4
            stg_f = [psb(f"stg_f{i}", [128, 4096]) for i in range(NS)]
            stg_b = [psb(f"stg_b{i}", [128, 4096], BF16) for i in range(NS)]
            wp_f = psb("wp_f", [128, 4, 128])
            b_stg_f = [Buf() for _ in range(NS)]
            b_stg_b = [Buf() for _ in range(NS)]
            s_ld = [Sem(nc, stack, f"d_pld{i}", 16) for i in range(NS)]
            s_st = [Sem(nc, stack, f"d_pst{i}", 16) for i in range(NS)]
            b_wpf = Buf()
            sp.op(lambda e: e.dma_start(out=wp_f[:], in_=w_pool.rearrange("g c d -> c g d")), writes=[b_wpf], sem=misc_sem())
            dve.op(lambda e: e.tensor_copy(out=wpool_bf[:], in_=wp_f[:]), reads=[b_wpf], writes=[b_wpool])

            def pload(b):
                s_ = b % NS
                src, a, n = blk_src[b]
                sp.op(lambda e: e.dma_start(out=stg_f[s_][:].rearrange("p (a n) -> p a n", a=a), in_=src),
                      writes=[b_stg_f[s_]], sem=s_ld[s_])

            NP1 = 5
            for b in range(min(NS, NP1)):
                pload(b)
            for b in range(NP1):
                s_ = b % NS
                act.op(lambda e: e.copy(out=stg_b[s_][:, 0:2048], in_=stg_f[s_][:, 0:2048]),
                       reads=[b_stg_f[s_]], writes=[b_stg_b[s_]])
                if b >= NS:
                    dve.wait_tok(scr_done[b - NS])
                t1 = dve.op(lambda e: e.tensor_copy(out=stg_b[s_][:, 2048:4096], in_=stg_f[s_][:, 2048:4096]),
                            reads=[b_stg_f[s_]])
                act.wait_tok(t1)
                tok = act.op(lambda e: e.dma_start(out=wscr[b], in_=stg_b[s_][:]), reads=[b_stg_b[s_]], sem=s_st[s_])
                scr_done.append(tok)
                if b + NS < NP1:
                    pload(b + NS)
            for q in (act, dve, pool, pe):
                for tok in scr_done[-NS:]:
                    q.wait_tok(tok)
            for tok in scr_done:
                sp.wait_tok(tok)
        scr_tok = {b: scr_done[b] for b in range(NP1)}

        wring = [sb(f"wring{i}", [128, 4096], BF16) for i in range(NB)]
        b_wring = [Buf(f"wring{i}") for i in range(NB)]
        s_wring = [Sem(nc, stack, f"d_w{i}", 16) for i in range(NB)]
        NXT = 4
        xtok = [sb(f"xtok{i}", [128, D]) for i in range(NXT)]
        b_xtok = [Buf() for _ in range(NXT)]
        s_xtok = [Sem(nc, stack, f"d_x{i}", 16) for i in range(NXT)]
        ytok = [sb(f"ytok{i}", [128, D]) for i in range(2)]
        b_ytok = [Buf(), Buf()]
        s_ytok = [Sem(nc, stack, f"d_y{i}", 16) for i in range(2)]
        xT = sb("xT", [128, 8, TT], BF16)
        b_xT = [Buf(f"xT{k}") for k in range(8)]
        mix = sb("mix", [128, 8, TT], BF16)
        b_mix = [Buf(f"mix{k}") for k in range(8)]
        hb = sb("hb", [128, 8, TT], BF16)
        b_hb = [Buf(f"hb{k}") for k in range(8)]
        XB = [sb(f"X{i}", [128, 8, TT]) for i in range(2)]
        b_XB = [[Buf(f"X{i}_{k}") for k in range(8)] for i in range(2)]
        rb = sb("rb", [128, 32, TT], BF16)
        b_rb = [Buf(f"rb{k}") for k in range(32)]
        uext = sb("uext", [128, 4, PH + TT])
        b_u = [Buf(f"u{k}") for k in range(4)]
        vext = sb("vext", [128, 4, CH + TT])
        b_v = [Buf(f"v{k}") for k in range(4)]
        acc = sb("acc", [128, 4, TT])
        b_acc = [Buf(f"acc{k}") for k in range(4)]
        dbf = sb("dbf", [128, 4, TT], BF16)
        b_d = [Buf(f"d{k}") for k in range(4)]
        ptmp = [sb(f"ptmp{i}", [128, PH + TT]) for i in range(2)]
        b_ptmp = [Buf(), Buf()]
        tmps = [sb(f"tmp{i}", [128, TT]) for i in range(NTMP)]
        b_tmps = [Buf(f"tmp{i}") for i in range(NTMP)]
        tmp_rr = [0]
        NRT = 2
        rtmps = [sb(f"rtmp{i}", [128, TT]) for i in range(NRT)]
        b_rtmps = [Buf(f"rtmp{i}") for i in range(NRT)]
        rtmp_rr = [0]
        NBT = 6
        btmps = [sb(f"btmp{i}", [128, TT], BF16) for i in range(NBT)]
        b_btmps = [Buf() for _ in range(NBT)]
        btmp_live = [False] * NBT
        btmp_rr = [0]
        _lnA = [sb(f"lnA{i}", [128, TT]) for i in range(1)]
        _lnB = [sb(f"lnB{i}", [128, TT]) for i in range(1)]
        _b_ln = [Buf(f"ln{i}") for i in range(1)]
        lnA = [_lnA[0], _lnA[0]]
        lnB = [_lnB[0], _lnB[0]]
        b_ln = [_b_ln[0], _b_ln[0]]
        hist_io = sb("hist_io", [32, 512])
        b_hist_io = Buf()
        print("SBUF bytes remaining per partition:", nc.sbuf_bytes_remaining)

        RING = [0, 1, 2, 3, 6, 7]

        def next_bank4():
            i = RING[bank_rr[0] % len(RING)]
            bank_rr[0] += 1
            return banks[i], bank_buf[i]

        def tmp():
            i = tmp_rr[0] % NTMP
            tmp_rr[0] += 1
            return tmps[i], b_tmps[i]

        def rtmp():
            i = rtmp_rr[0] % NRT
            rtmp_rr[0] += 1
            return rtmps[i], b_rtmps[i]

        def btmp():
            for _ in range(NBT):
                i = btmp_rr[0] % NBT
                btmp_rr[0] += 1
                if not btmp_live[i]:
                    break
            else:
                raise AssertionError("all bf16 temps hold chunks whose consumer is not emitted yet")
            btmp_live[i] = True
            return btmps[i], b_btmps[i], i

        class Stream:
            def __init__(self):
                self.order = []
                self.issued = 0
                self.opened = 0
                self.open = {}
                self.closed = set()
                self.closed_upto = 0

            def _issue_upto(self, n):
                while self.issued < min(n, len(self.order)):
                    j = self.issued
                    s_ = j % NB
                    blk = self.order[j]
                    sp.wait_tok(scr_tok[blk])
                    sp.op(lambda e: e.dma_start(out=wring[s_][:], in_=wscr[blk]), writes=[b_wring[s_]], sem=s_wring[s_])
                    self.issued += 1

            def use(self, blk):
                if blk not in self.open:
                    j = self.opened
                    assert self.order[j] == blk, (j, self.order[j], blk)
                    self._issue_upto(max(self.issued, min(self.closed_upto + NB, j + 1)))
                    assert j < self.issued, "weight ring too shallow for this consumption pattern"
                    self.open[blk] = j
                    self.opened += 1
                j = self.open[blk]
                return wring[j % NB], b_wring[j % NB]

            def close(self, blk):
                j = self.open.pop(blk)
                self.closed.add(j)
                while self.closed_upto in self.closed:
                    self.closed.discard(self.closed_upto)
                    self.closed_upto += 1
                self._issue_upto(self.closed_upto + NB)

        stream = Stream()

        def mm_group(out_ap, out_buf, items):
            n = len(items)
            rbufs = []
            for i, (l, r, bufs) in enumerate(items):
                if i == 0:
                    pe.deps(reads=bufs, writes=[out_buf])
                else:
                    pe.deps(reads=bufs)
                ins = pe.eng.matmul(out_ap, l, r, start=(i == 0), stop=(i == n - 1))
                for b in bufs:
                    if b not in rbufs:
                        rbufs.append(b)
            return pe.commit(ins, reads=rbufs, writes=[out_buf])

        class LNAcc:
            def __init__(self, which, nchunks, T):
                self.which, self.n, self.T = which, nchunks, T
                bi = 6 if which == 2 else 4
                self.s1, self.b1 = banks[bi], bank_buf[bi]
                self.s2, self.b2 = banks[bi + 1], bank_buf[bi + 1]
                self.k = 0
                self.pend = []

            def prep(self, src_ap, src_buf):
                T = self.T
                cb, bcb, ic = btmp()
                qb, bqb, iq = btmp()
                act.op(lambda e: e.copy(out=cb[:, :T], in_=src_ap), reads=[src_buf], writes=[bcb])
                act.op(lambda e: e.activation(out=qb[:, :T], in_=src_ap, func=AF.Square), reads=[src_buf], writes=[bqb])
                self.pend.append((cb, bcb, qb, bqb, ic, iq))

            def add_pending(self, keep=0):
                while len(self.pend) > keep:
                    self._add(self.pend.pop(0))

            def _add(self, ops):
                cb, bcb, qb, bqb, ic, iq = ops
                T = self.T
                k = self.k
                first, last = (k == 0), (k == self.n - 1)
                pe.deps(reads=[bcb] + CONST, writes=[self.b1] if first else [])
                i1 = pe.eng.matmul(self.s1[:, :T], ones_bf[:], cb[:, :T], start=first, stop=last)
                pe.commit(i1, reads=[bcb], writes=[self.b1] if last else [])
                pe.deps(reads=[bqb], writes=[self.b2] if first else [])
                i2 = pe.eng.matmul(self.s2[:, :T], ones_bf[:], qb[:, :T], start=first, stop=last)
                pe.commit(i2, reads=[bqb], writes=[self.b2] if last else [])
                btmp_live[ic] = False
                btmp_live[iq] = False
                self.k += 1

            def finish(self):
                self.add_pending()
                assert self.k == self.n
                T = self.T
                nf = float(self.n * 128)
                A, Bc, bl = lnA[self.which], lnB[self.which], b_ln[self.which]
                q, bq = tmp()
                act.op(lambda e: e.activation(out=q[:, :T], in_=self.s1[:, :T], func=AF.Square, scale=1.0 / nf),
                       reads=[self.b1], writes=[bq])
                dve.op(lambda e: e.scalar_tensor_tensor(out=q[:, :T], in0=self.s2[:, :T], scalar=1.0 / nf, in1=q[:, :T],
                                                        op0=ALU.mult, op1=ALU.subtract), reads=[self.b2, bq], writes=[bq])
                act.op(lambda e: e.activation(out=q[:, :T], in_=q[:, :T], func=AF.Ln, bias=epsc[:, 0:1], scale=1.0),
                       reads=[bq, b_ones], writes=[bq])
                act.op(lambda e: e.activation(out=A[:, :T], in_=q[:, :T], func=AF.Exp, scale=-0.5),
                       reads=[bq], writes=[bl])

            def center(self, src_ap, src_buf):
                T = self.T
                nf = float(self.n * 128)
                dve.op(lambda e: e.scalar_tensor_tensor(out=src_ap, in0=self.s1[:, :T], scalar=-1.0 / nf, in1=src_ap,
                                                        op0=ALU.mult, op1=ALU.add), reads=[self.b1, src_buf], writes=[src_buf])

        def ln_apply(which, src_ap, src_buf, T):
            A, bl = lnA[which], b_ln[which]
            t1, bt1 = tmp()
            dve.op(lambda e: e.tensor_tensor(out=t1[:, :T], in0=src_ap, in1=A[:, :T], op=ALU.mult),
                   reads=[src_buf, bl], writes=[bt1])
            return t1, bt1

        def hist_zero():
            for c in range(4):
                pool.op(lambda e: e.memset(uext[:, c, 0:PH], 0.0), writes=[b_u[c]])
                pool.op(lambda e: e.memset(vext[:, c, 0:CH], 0.0), writes=[b_v[c]])

        def hist_load(src, n, dst, bdst):
            sp.op(lambda e: e.dma_start(out=hist_io[0:n, :], in_=src), writes=[b_hist_io], sem=misc_sem())
            bk, bb = next_bank4()
            pe.deps(reads=[b_hist_io] + CONST, writes=[bb])
            for c in range(4):
                ins = pe.eng.transpose(bk[:, c * 32:c * 32 + n], hist_io[0:n, c * 128:(c + 1) * 128], ident[0:n, 0:n])
            pe.commit(ins, reads=[b_hist_io], writes=[bb])
            for c in range(4):
                act.op(lambda e: e.copy(out=dst[:, c, 0:n], in_=bk[:, c * 32:c * 32 + n]), reads=[bb], writes=[bdst[c]])

        def hist_store(srct, bsrc, T, n, dst_dram):
            bk, bb = next_bank4()
            pe.deps(reads=list(bsrc) + CONST, writes=[bb])
            for c in range(4):
                ins = pe.eng.transpose(bk[0:n, c * 128:(c + 1) * 128], srct[:, c, T:T + n], ident[:, :])
            pe.commit(ins, reads=list(bsrc), writes=[bb])
            act.op(lambda e: e.copy(out=hist_io[0:n, :], in_=bk[0:n, :]), reads=[bb], writes=[b_hist_io])
            return sp.op(lambda e: e.dma_start(out=dst_dram, in_=hist_io[0:n, :]), reads=[b_hist_io], sem=misc_sem())

        sf_all = rb[:].rearrange("p a n -> p (a n)").bitcast(F32)
        sbf_all = XB[1][:].rearrange("p a n -> p (a n)").bitcast(BF16)
        SF = [sf_all[:, i * 4096:(i + 1) * 4096] for i in range(2)]
        SBF = [sbf_all[:, i * 4096:(i + 1) * 4096] for i in range(2)]
        b_SF = [Buf(), Buf()]
        b_SBF = [Buf(), Buf()]
        s_ld2 = [Sem(nc, stack, f"d_p2ld{i}", 16) for i in range(2)]
        s_st2 = [Sem(nc, stack, f"d_p2st{i}", 16) for i in range(2)]

        def p2_load(k):
            b = NP1 + k
            if b >= NBLK:
                return
            src, a, n = blk_src[b]
            sp.op(lambda e: e.dma_start(out=SF[k % 2].rearrange("p (a n) -> p a n", a=a), in_=src),
                  writes=[b_SF[k % 2]], sem=s_ld2[k % 2])

        def p2_step(k):
            b = NP1 + k
            if b >= NBLK:
                return
            act.op(lambda e: e.copy(out=SBF[k % 2], in_=SF[k % 2]), reads=[b_SF[k % 2]], writes=[b_SBF[k % 2]])
            scr_tok[b] = act.op(lambda e: e.dma_start(out=wscr[b], in_=SBF[k % 2]), reads=[b_SBF[k % 2]], sem=s_st2[k % 2])
            p2_load(k + 2)

        def p2_barrier():
            for b in range(NBLK - 2, NBLK):
                for q in (act, dve, pe, pool):
                    q.wait_tok(scr_tok[b])

        xcnt = [0]
        ycnt = [0]
        final_toks = []

        class Tile:
            def __init__(self, gi, x_dram, y_dram, T, first):
                self.gi, self.x_dram, self.y_dram, self.T, self.first = gi, x_dram, y_dram, T, first
                self.X = XB[gi % 2]
                self.bX = b_XB[gi % 2]
                self.nchunk = max(1, T // 128)
                self.rows = min(T, 128)
                self.ln1 = None
                self.ln2 = None
                self.conv_pos = 0
                self.xslots = []

            def xload(self):
                self.xslots = []
                for tc in range(self.nchunk):
                    xs_ = xcnt[0] % NXT
                    xcnt[0] += 1
                    self.xslots.append(xs_)
                    sp.op(lambda e: e.dma_start(out=xtok[xs_][0:self.rows, :], in_=self.x_dram[tc * 128:tc * 128 + self.rows, :]),
                          writes=[b_xtok[xs_]], sem=s_xtok[xs_])

            def A1(self):
                self.A1a()
                self.A1b()
                self.A1c()

            def A1a(self):
                T, rows, X, bX = self.T, self.rows, self.X, self.bX
                for tc in range(self.nchunk):
                    xs_ = self.xslots[tc]
                    for half in range(2):
                        bk, bb = next_bank4()
                        pe.deps(reads=[b_xtok[xs_]] + CONST, writes=[bb])
                        for j in range(4):
                            fc = half * 4 + j
                            ins = pe.eng.transpose(bk[:, j * 128:j * 128 + rows],
                                                   xtok[xs_][0:rows, fc * 128:(fc + 1) * 128], ident[0:rows, 0:rows])
                        pe.commit(ins, reads=[b_xtok[xs_]], writes=[bb])
                        src = bk[:].rearrange("p (j t) -> p j t", j=4)[:, :, 0:rows]
                        act.op(lambda e: e.copy(out=xT[:, half * 4:half * 4 + 4, tc * 128:tc * 128 + rows], in_=src),
                               reads=[bb], writes=b_xT[half * 4:half * 4 + 4])
                        act.op(lambda e: e.activation(out=X[:, half * 4:half * 4 + 4, tc * 128:tc * 128 + rows], in_=src,
                                                      func=AF.Identity, bias=0.0, scale=ALPHA),
                               reads=[bb], writes=bX[half * 4:half * 4 + 4])

            def A1b(self):
                T, rows, X, bX = self.T, self.rows, self.X, self.bX

                def inproj(mc):
                    blk, bblk = stream.use(mc // 4)
                    w3 = blk[:].rearrange("p (a n) -> p a n", a=8)
                    bk, bb = next_bank4()
                    items = [(w3[:, kc, (mc % 4) * 128:(mc % 4 + 1) * 128], xT[:, kc, 0:T], [bblk, b_xT[kc]])
                             for kc in range(8)]
                    mm_group(bk[:, :T], bb, items)
                    return bk, bb

                for c in range(4):
                    bkg, bbg = inproj(8 + c)
                    sg, bsg = tmp()
                    act.op(lambda e: e.activation(out=sg[:, :T], in_=bkg[:, :T], func=AF.Sigmoid,
                                                  bias=pcol(P_BIN, 8 + c), scale=1.0), reads=[bbg] + CONST, writes=[bsg])
                    bka, bba = inproj(4 + c)
                    dve.op(lambda e: e.scalar_tensor_tensor(out=vext[:, c, CH:CH + T], in0=bka[:, :T], scalar=pcol(P_BIN, 4 + c),
                                                            in1=sg[:, :T], op0=ALU.add, op1=ALU.mult),
                           reads=[bba, bsg] + CONST, writes=[b_v[c]])
                stream.close(2)
                stream.close(1)
                for c in range(4):
                    bk, bb = inproj(c)
                    act.op(lambda e: e.activation(out=uext[:, c, PH:PH + T], in_=bk[:, :T], func=AF.Identity,
                                                  bias=pcol(P_BIN, c), scale=1.0), reads=[bb] + CONST, writes=[b_u[c]])
                stream.close(0)

                for g, w in enumerate(WINS):
                    L = PH + T
                    cur, bcur = uext[:, g, :], b_u[g]
                    sh = 1
                    pi = 0
                    while sh < w:
                        dst, bdst = ptmp[pi % 2], b_ptmp[pi % 2]
                        pi += 1
                        lo = 2 * sh - 1
                        src_ap = cur
                        dve.op(lambda e: e.tensor_tensor(out=dst[:, lo:L], in0=src_ap[:, lo:L], in1=src_ap[:, lo - sh:L - sh],
                                                         op=ALU.add), reads=[bcur], writes=[bdst])
                        cur, bcur = dst, bdst
                        sh *= 2
                    sw = cur
                    dve.op(lambda e: e.scalar_tensor_tensor(out=dbf[:, g, 0:T], in0=sw[:, PH:PH + T], scalar=1.0 / w,
                                                            in1=uext[:, g, PH:PH + T], op0=ALU.mult, op1=ALU.subtract),
                           reads=[bcur, b_u[g]], writes=[b_d[g]])
                    if self.first:
                        nfix = w - 1
                        t_, bt_ = tmp()
                        dve.op(lambda e: e.tensor_tensor(out=t_[:, 0:nfix], in0=sw[:, PH:PH + nfix],
                                                         in1=rcnt[:, g * 16:g * 16 + nfix], op=ALU.mult),
                               reads=[bcur] + CONST, writes=[bt_])
                        dve.op(lambda e: e.tensor_tensor(out=dbf[:, g, 0:nfix], in0=t_[:, 0:nfix],
                                                         in1=uext[:, g, PH:PH + nfix], op=ALU.subtract),
                               reads=[bt_, b_u[g]], writes=[b_d[g]])
            def A1c(self):
                T = self.T
                for g in range(4):
                    bk, bb = next_bank4()
                    mm_group(bk[:, :T], bb, [(wpool_bf[:, g, :], dbf[:, g, 0:T], [b_d[g]] + CONST)])
                    act.op(lambda e: e.activation(out=mix[:, g, 0:T], in_=bk[:, :T], func=AF.Identity,
                                                  bias=0.0, scale=pcol(P_PSC, g)), reads=[bb] + CONST, writes=[b_mix[g]])

            def conv(self, n):
                T = self.T
                for _ in range(n):
                    if self.conv_pos >= 4 * KC:
                        return
                    k, c = divmod(self.conv_pos, 4)
                    self.conv_pos += 1
                    wk = params[:, P_CW + c * KC + k:P_CW + c * KC + k + 1]
                    if k == 0:
                        dve.op(lambda e: e.tensor_scalar(out=acc[:, c, 0:T], in0=vext[:, c, 0:T], scalar1=wk,
                                                         scalar2=pcol(P_CB, c), op0=ALU.mult, op1=ALU.add),
                               reads=[b_v[c]] + CONST, writes=[b_acc[c]])
                    else:
                        dve.op(lambda e: e.scalar_tensor_tensor(out=acc[:, c, 0:T], in0=vext[:, c, k:k + T], scalar=wk,
                                                                in1=acc[:, c, 0:T], op0=ALU.mult, op1=ALU.add),
                               reads=[b_v[c], b_acc[c]] + CONST, writes=[b_acc[c]])

            def A2(self):
                T = self.T
                ln = LNAcc(0, 4, T)
                for c in range(4):
                    ln.prep(acc[:, c, 0:T], b_acc[c])
                    ln.add_pending()
                ln.finish()
                for c in range(4):
                    ln.center(acc[:, c, 0:T], b_acc[c])
                for c in range(4):
                    t2, bt2 = ln_apply(0, acc[:, c, 0:T], b_acc[c], T)
                    yv, byv = tmp()
                    sg, bsg = tmp()
                    act.op(lambda e: e.activation(out=yv[:, :T], in_=t2[:, :T], func=AF.Identity,
                                                  bias=pcol(P_CLB, c), scale=pcol(P_CLG, c)), reads=[bt2] + CONST, writes=[byv])
                    act.op(lambda e: e.activation(out=sg[:, :T], in_=t2[:, :T], func=AF.Sigmoid,
                                                  bias=pcol(P_CLB, c), scale=pcol(P_CLG, c)), reads=[bt2] + CONST, writes=[bsg])
                    dve.op(lambda e: e.tensor_tensor(out=mix[:, 4 + c, 0:T], in0=yv[:, :T], in1=sg[:, :T], op=ALU.mult),
                           reads=[byv, bsg], writes=[b_mix[4 + c]])
                if T == TT:
                    for c in range(4):
                        act.op(lambda e: e.copy(out=uext[:, c, 0:PH], in_=uext[:, c, T:T + PH]), reads=[b_u[c]], writes=[b_u[c]])
                        act.op(lambda e: e.copy(out=vext[:, c, 0:CH], in_=vext[:, c, T:T + CH]), reads=[b_v[c]], writes=[b_v[c]])

            def A3(self):
                for mc in range(8):
                    self.A3g(mc)

            def A3g(self, mc):
                T, X, bX = self.T, self.X, self.bX
                if mc == 0:
                    self.ln1 = LNAcc(1, 8, T)
                ln = self.ln1
                if True:
                    blk, bblk = stream.use(3 + mc // 4)
                    w3 = blk[:].rearrange("p (a n) -> p a n", a=8)
                    bk, bb = next_bank4()
                    items = [(w3[:, kc, (mc % 4) * 128:(mc % 4 + 1) * 128], mix[:, kc, 0:T], [bblk, b_mix[kc]])
                             for kc in range(8)]
                    mm_group(bk[:, :T], bb, items)
                    dve.op(lambda e: e.scalar_tensor_tensor(out=X[:, mc, 0:T], in0=bk[:, :T], scalar=pcol(P_BOUT, mc),
                                                            in1=X[:, mc, 0:T], op0=ALU.add, op1=ALU.add),
                           reads=[bb, bX[mc]] + CONST, writes=[bX[mc]])
                    ln.prep(X[:, mc, 0:T], bX[mc])
                    ln.add_pending(keep=2)
                    if mc % 4 == 3:
                        stream.close(3 + mc // 4)
                if mc == 7:
                    ln.add_pending()

            def A4(self):
                self.A4fin()
                self.A4norm()

            def A4fin(self):
                self.ln1.finish()
                for mc in range(8):
                    self.ln1.center(self.X[:, mc, 0:self.T], self.bX[mc])

            def A4norm(self):
                T, X, bX = self.T, self.X, self.bX
                for mc in range(8):
                    t2, bt2 = ln_apply(1, X[:, mc, 0:T], bX[mc], T)
                    act.op(lambda e: e.activation(out=hb[:, mc, 0:T], in_=t2[:, :T], func=AF.Identity,
                                                  bias=pcol(P_B1, mc), scale=pcol(P_G1, mc)), reads=[bt2] + CONST, writes=[b_hb[mc]])
                    act.op(lambda e: e.activation(out=X[:, mc, 0:T], in_=t2[:, :T], func=AF.Identity,
                                                  bias=dpar[:, 8 + mc:9 + mc], scale=dpar[:, mc:mc + 1]),
                           reads=[bt2] + CONST, writes=[bX[mc]])

            def up(self, mc):
                T = self.T
                blk, bblk = stream.use(5 + mc // 4)
                w3 = blk[:].rearrange("p (a n) -> p a n", a=8)
                bk, bb = next_bank4()
                items = [(w3[:, kc, (mc % 4) * 128:(mc % 4 + 1) * 128], hb[:, kc, 0:T], [bblk, b_hb[kc]])
                         for kc in range(8)]
                mm_group(bk[:, :T], bb, items)
                r1, br1 = rtmp()
                act.op(lambda e: e.activation(out=r1[:, :T], in_=bk[:, :T], func=AF.Relu), reads=[bb], writes=[br1])
                act.op(lambda e: e.activation(out=rb[:, mc, 0:T], in_=r1[:, :T], func=AF.Square),
                       reads=[br1], writes=[b_rb[mc]])
                if mc % 4 == 3:
                    stream.close(5 + mc // 4)

            def down(self, mc):
                T, X, bX = self.T, self.X, self.bX
                blk, bblk = stream.use(13 + mc)
                w3 = blk[:].rearrange("p (a n) -> p a n", a=32)
                bk, bb = next_bank4()
                items = [(w3[:, kc, :], rb[:, kc, 0:T], [bblk, b_rb[kc]]) for kc in range(32)]
                mm_group(bk[:, :T], bb, items)
                stream.close(13 + mc)
                dve.op(lambda e: e.tensor_tensor(out=X[:, mc, 0:T], in0=bk[:, :T], in1=X[:, mc, 0:T], op=ALU.add),
                       reads=[bb, bX[mc]], writes=[bX[mc]])

            def Fend(self):
                for tc in range(self.nchunk):
                    self.Fout(tc)

            def Fout(self, tc):
                T, X, bX, rows = self.T, self.X, self.bX, self.rows
                ys_ = ycnt[0] % 2
                ycnt[0] += 1
                st = ystat[:, ys_, :]
                bst = b_ystat[ys_]
                hb_ = []
                for half in range(2):
                    bk, bb = next_bank4()
                    pe.deps(reads=bX[half * 4:half * 4 + 4] + CONST, writes=[bb])
                    for j in range(4):
                        fc = half * 4 + j
                        ins = pe.eng.transpose(bk[0:rows, j * 128:(j + 1) * 128], X[:, fc, tc * 128:tc * 128 + rows], ident[:, :])
                    pe.commit(ins, reads=bX[half * 4:half * 4 + 4], writes=[bb])
                    dve.op(lambda e: e.bn_stats(out=st[0:rows, half * 6:half * 6 + 6], in_=bk[0:rows, :]),
                           reads=[bb], writes=[bst])
                    hb_.append((bk, bb))
                dve.op(lambda e: e.bn_aggr(out=st[0:rows, 12:14], in_=st[0:rows, 0:12]), reads=[bst], writes=[bst])
                act.op(lambda e: e.activation(out=st[0:rows, 14:15], in_=st[0:rows, 13:14], func=AF.Ln,
                                              bias=epsc[0:rows, 0:1], scale=1.0), reads=[bst, b_ones], writes=[bst])
                act.op(lambda e: e.activation(out=st[0:rows, 15:16], in_=st[0:rows, 14:15], func=AF.Exp, scale=-0.5),
                       reads=[bst], writes=[bst])
                dve.op(lambda e: e.scalar_tensor_tensor(out=st[0:rows, 16:17], in0=st[0:rows, 12:13], scalar=-1.0,
                                                        in1=st[0:rows, 15:16], op0=ALU.mult, op1=ALU.mult),
                       reads=[bst], writes=[bst])
                for half, (bk, bb) in enumerate(hb_):
                    act.op(lambda e: e.activation(out=ytok[ys_][0:rows, half * 512:(half + 1) * 512], in_=bk[0:rows, :],
                                                  func=AF.Identity, bias=st[0:rows, 16:17], scale=st[0:rows, 15:16]),
                           reads=[bb, bst], writes=[b_ytok[ys_]])
                dve.op(lambda e: e.tensor_tensor(out=ytok[ys_][0:rows, :], in0=ytok[ys_][0:rows, :], in1=g2bc[0:rows, :],
                                                 op=ALU.mult), reads=[b_ytok[ys_]] + CONST, writes=[b_ytok[ys_]])
                dve.op(lambda e: e.tensor_tensor(out=ytok[ys_][0:rows, :], in0=ytok[ys_][0:rows, :], in1=b2bc[0:rows, :],
                                                 op=ALU.add), reads=[b_ytok[ys_]] + CONST, writes=[b_ytok[ys_]])
                tok = act.op(lambda e: e.dma_start(out=self.y_dram[tc * 128:tc * 128 + rows, :], in_=ytok[ys_][0:rows, :]),
                             reads=[b_ytok[ys_]], sem=s_ytok[ys_])
                final_toks.append(tok)

        tiles = [Tile(i, xp[i * TT:(i + 1) * TT, :], yp[i * TT:(i + 1) * TT, :], TT, first=(i == 0)) for i in range(n_ptiles)]
        stile = Tile(n_ptiles, xs, ys, 16, first=False)
        items = []
        A_BLK = [2, 1, 0]
        items.append(("hist_zero",))
        items.append(("xload", tiles[0]))
        items.append(("prefetch",))
        items.append(("p2_load", 0))
        items.append(("p2_load", 1))
        items.append(("A1a", tiles[0]))
        items.append(("p2_step", 0))
        items.append(("p2_step", 1))
        items.append(("A1b", tiles[0]))
        items.append(("p2_step", 2))
        items.append(("p2_step", 3))
        items.append(("A1c", tiles[0]))
        tiles.append(stile)
        items.append(("xload", tiles[1]))
        for k in range(8):
            items.append(("p2_step", 4 + k))
            items.append(("conv", tiles[0], 16))
        items.append(("A2", tiles[0]))
        items.append(("p2_step", 12))
        items.append(("p2_step", 13))
        for mc in range(8):
            items.append(("A3g", tiles[0], mc))
            if mc == 1:
                items.append(("p2_step", 14))
            if mc == 4:
                items.append(("p2_step", 15))
        items.append(("A4", tiles[0]))
        items.append(("p2_barrier",))
        for i in range(1, n_ptiles + 1):
            t, p = tiles[i], tiles[i - 1]
            nxt = tiles[i + 1] if i + 1 <= n_ptiles else None
            if i == 1:
                items.append(("A1a", t))
            if t is stile:
                items.append(("hist_swap",))
            items.append(("A1b", t))
            if nxt is not None:
                items.append(("xload", nxt))
            for k in range(32):
                items.append(("up", p, k))
                items.append(("conv", t, 3))
                if k == 5:
                    items.append(("A1c", t))
            for k in range(5):
                items.append(("down", p, k))
                items.append(("conv", t, 6))
            items.append(("A2", t))
            for k in range(5, 8):
                items.append(("down", p, k))
            for mc in range(8):
                items.append(("A3g", t, mc))
                if mc % 2 == 1 and mc // 2 < p.nchunk:
                    items.append(("Fout", p, mc // 2))
            items.append(("A4fin", t))
            if nxt is not None:
                items.append(("A1a", nxt))
            items.append(("A4norm", t))
        p = tiles[-1]
        items += [("up", p, k) for k in range(32)] + [("down", p, k) for k in range(8)] + [("Fend", p)]
        items.append(("hist_final",))

        for it in items:
            if it[0] in ("A1", "A1b"):
                stream.order += A_BLK
            elif it[0] == "A3":
                stream.order += [3, 4]
            elif it[0] == "A3g" and it[2] % 4 == 0:
                stream.order.append(3 + it[2] // 4)
            elif it[0] == "up" and it[2] % 4 == 0:
                stream.order.append(5 + it[2] // 4)
            elif it[0] == "down":
                stream.order.append(13 + it[2])

        for it in items:
            kind = it[0]
            if kind == "hist_zero":
                hist_zero()
            elif kind == "prefetch":
                stream._issue_upto(NB)
            elif kind == "p2_load":
                p2_load(it[1])
            elif kind == "p2_step":
                p2_step(it[1])
            elif kind == "p2_barrier":
                p2_barrier()
            elif kind == "hist_swap":
                final_toks.append(hist_store(uext, b_u, TT, PH, o_pp))
                final_toks.append(hist_store(vext, b_v, TT, CH, o_cp))
                hist_load(sp_in, PH, uext, b_u)
                hist_load(sc_in, CH, vext, b_v)
            elif kind == "hist_final":
                final_toks.append(hist_store(uext, b_u, 16, PH, o_ps))
                final_toks.append(hist_store(vext, b_v, 16, CH, o_cs))
            elif kind in ("conv", "up", "down", "A3g", "Fout"):
                getattr(it[1], kind)(it[2])
            else:
                getattr(it[1], kind)()
        assert stream.opened == len(stream.order) and not stream.open

        for tok in final_toks:
            sp.wait_tok(tok)
        for q in (pe, act, dve, pool):
            if q.sem.n:
                sp.wait_tok((q.sem, q.sem.n))
    return nc


def _pack_params(b_in, pool_scale, conv_w, conv_b, conv_ln_g, conv_ln_b, b_out, ln1_g, ln1_b, ln2_g, ln2_b):
    def cols(v):
        v = np.asarray(v, np.float32).reshape(-1, 128)
        return v.T
    cw = np.asarray(conv_w, np.float32).reshape(KC, 4, 128)
    cw = np.transpose(cw, (2, 1, 0)).reshape(128, 4 * KC)
    parts = [cols(b_in), cols(pool_scale), cols(conv_b), cols(conv_ln_g), cols(conv_ln_b), cols(b_out),
             cols(ln1_g), cols(ln1_b), cols(ln2_g), cols(ln2_b), cw]
    out = np.ascontiguousarray(np.concatenate(parts, axis=1), dtype=np.float32)
    assert out.shape == (128, NPAR), out.shape
    return out


def _consts():
    ident = np.eye(128, dtype=np.float32)
    rc = np.zeros((4, 16), np.float32)
    for g, w in enumerate(WINS):
        for t in range(16):
            rc[g, t] = 1.0 / min(t + 1, w)
    rcnt = np.ascontiguousarray(np.broadcast_to(rc.reshape(1, 64), (128, 64)), dtype=np.float32)
    return ident, rcnt


_NC_CACHE = {}


def run(x_prompt, x_sample, state_pool, state_conv, w_in, b_in, w_pool, pool_scale, conv_w, conv_b, conv_ln_g,
        conv_ln_b, w_out, b_out, ln1_g, ln1_b, w_up, w_down, ln2_g, ln2_b, cfg=None, trace=False):
    f = lambda a: np.ascontiguousarray(np.asarray(a, dtype=np.float32))
    x_prompt = f(x_prompt)
    nb, seq, _ = x_prompt.shape
    assert seq % TT == 0
    n_ptiles = seq // TT
    key = (n_ptiles, repr(sorted((cfg or {}).items())))
    if key not in _NC_CACHE:
        _NC_CACHE[key] = build(n_ptiles, cfg)
    nc = _NC_CACHE[key]
    params = _pack_params(b_in[0], pool_scale[0], conv_w[0], conv_b[0], conv_ln_g[0], conv_ln_b[0], b_out[0],
                          ln1_g[0], ln1_b[0], ln2_g[0], ln2_b[0])
    ident, rcnt = _consts()
    shared = {"w_in": f(w_in[0]), "w_out": f(w_out[0]), "w_up": f(w_up[0]), "w_down": f(w_down[0]),
              "w_pool": f(w_pool[0]), "params": params, "ident": ident, "rcnt": rcnt,
              "ln2g": f(ln2_g[0]).reshape(1, D), "ln2b": f(ln2_b[0]).reshape(1, D)}
    x_sample, state_pool, state_conv = f(x_sample), f(state_pool), f(state_conv)
    in_maps = []
    for c in range(8):
        m = dict(shared)
        m["xp"] = x_prompt[c]
        m["xs"] = x_sample[c]
        m["sp"] = state_pool[0, c]
        m["sc"] = state_conv[0, c]
        in_maps.append(m)
    res = run_bass_kernel_spmd(nc, in_maps, core_ids=list(range(8)), trace=trace)
    R = res.results
    stk = lambda k: np.stack([np.asarray(R[c][k], dtype=np.float32) for c in range(8)], axis=0)
    outs = (stk("yp"), stk("ys"), stk("o_pp")[None], stk("o_cp")[None], stk("o_ps")[None], stk("o_cs")[None])
    return outs, res


def kernel(**inputs):
    outs, _ = run(**inputs)
    return outs
```
